# Optimizing a Trainium2 kernel written in Bass

```python
import math
import jax, jax.numpy as jnp
from jax import lax
import numpy as np

D_MODEL = 2048
BATCH = 8
SEQ = 4096
DEPTH = 2

DIFF_HEADS = 4
DIFF_DQK = 128
DIFF_DV = 2 * DIFF_DQK
ROT_DIM = DIFF_DQK // 4
ROPE_THETA = 500000.0
Q_BLOCK = 128
GLA_HEADS = 4
GLA_DK = 128
GLA_DV = 256
GLA_GATE_RANK = 16
GLA_TAU = 16.0
GLA_CHUNK = 64
D_FF = 5632
CONV_WIDTH = 3
MAX_POS_OFFSET = 1024
NORM_EPS = 1e-6

IN_SIZES = (
    DIFF_HEADS * 2 * DIFF_DQK,
    DIFF_HEADS * 2 * DIFF_DQK,
    DIFF_HEADS * DIFF_DV,
    GLA_HEADS * GLA_DK,
    GLA_HEADS * GLA_DK,
    GLA_HEADS * GLA_DV,
    GLA_HEADS * GLA_DV,
    GLA_GATE_RANK,
    GLA_GATE_RANK,
    D_MODEL,
    D_MODEL,
)
IN_COLS = sum(IN_SIZES)

kernel_name = "hybrid_diffattn_gla_convffn_adaln"


def rms_norm(x, g):
    xf = x.astype(jnp.float32)
    y = xf * lax.rsqrt(jnp.mean(xf * xf, axis=-1, keepdims=True) + NORM_EPS)
    return (y * g).astype(x.dtype)


def rotary_tables(positions):
    inv_freq = ROPE_THETA ** (-jnp.arange(0, ROT_DIM, 2, dtype=jnp.float32) / ROT_DIM)
    ang = positions.astype(jnp.float32)[..., None] * inv_freq
    return jnp.cos(ang)[:, :, None, None, :], jnp.sin(ang)[:, :, None, None, :]


def apply_partial_rotary(t, cos, sin):
    half = ROT_DIM // 2
    x1, x2, rest = t[..., :half], t[..., half:ROT_DIM], t[..., ROT_DIM:]
    rot = jnp.concatenate([x1 * cos - x2 * sin, x2 * cos + x1 * sin], axis=-1)
    return jnp.concatenate([rot.astype(t.dtype), rest], axis=-1)


def diff_attention(q, k, v, lam):
    B, S, H, _, dqk = q.shape
    nb = S // Q_BLOCK
    qb = jnp.moveaxis(q.reshape(B, nb, Q_BLOCK, H, 2, dqk), 1, 0)
    scale = dqk ** -0.5

    def block(qblk):
        s = jnp.einsum('bqhmd,bkhmd->bhmqk', qblk, k).astype(jnp.float32) * scale
        p = jax.nn.softmax(s, axis=-1)
        w = p[:, :, 0] - lam * p[:, :, 1]
        return jnp.einsum('bhqk,bkhe->bqhe', w.astype(v.dtype), v)

    out = lax.map(block, qb)
    return jnp.moveaxis(out, 0, 1).reshape(B, S, H, v.shape[-1])


def gla_scan(q, k, v, log_a):
    B, S, H, dk = q.shape
    dv = v.shape[-1]
    nc = S // GLA_CHUNK

    def chunks(t):
        return t.reshape(B, nc, GLA_CHUNK, H, t.shape[-1]).transpose(1, 0, 3, 2, 4)

    lower = jnp.tril(jnp.ones((GLA_CHUNK, GLA_CHUNK), dtype=bool))[:, :, None]

    def step(state, inp):
        qc, kc, vc, ac = inp
        b = jnp.cumsum(ac, axis=2)
        b_last = b[:, :, -1:, :]
        o_inter = jnp.einsum('bhcd,bhde->bhce', qc * jnp.exp(b), state)
        rel = b[:, :, :, None, :] - b[:, :, None, :, :]
        decay = jnp.exp(jnp.where(lower, rel, -jnp.inf))
        scores = jnp.einsum('bhid,bhjd,bhijd->bhij', qc, kc, decay)
        o_intra = jnp.einsum('bhij,bhje->bhie', scores, vc)
        state = (state * jnp.exp(b_last[:, :, 0, :])[..., None]
                 + jnp.einsum('bhjd,bhje->bhde', kc * jnp.exp(b_last - b), vc))
        return state, o_inter + o_intra

    state0 = jnp.zeros((B, H, dk, dv), jnp.float32)
    _, out = lax.scan(step, state0, (chunks(q), chunks(k), chunks(v), chunks(log_a)))
    return out.transpose(1, 0, 3, 2, 4).reshape(B, S, H, dv).astype(v.dtype)


def token_mixer(h, cos, sin, layer, w_in, lq1, lk1, lq2, lk2, subln_g,
                w2f, bf, w2b, bb, gla_g, w_bd, w_bg, w_o):
    B, S, _ = h.shape
    split_points = [int(p) for p in np.cumsum(IN_SIZES)[:-1]]
    (dq, dk, dv, gq, gk, gv, gr, glf, glb, gate_a, gate_b) = jnp.split(h @ w_in, split_points, axis=-1)

    q = apply_partial_rotary(dq.reshape(B, S, DIFF_HEADS, 2, DIFF_DQK), cos, sin)
    k = apply_partial_rotary(dk.reshape(B, S, DIFF_HEADS, 2, DIFF_DQK), cos, sin)
    v = dv.reshape(B, S, DIFF_HEADS, DIFF_DV)
    lam_init = 0.8 - 0.6 * math.exp(-0.3 * layer)
    lam = (jnp.exp(jnp.sum(lq1 * lk1).astype(jnp.float32))
           - jnp.exp(jnp.sum(lq2 * lk2).astype(jnp.float32)) + lam_init)
    o_d = rms_norm(diff_attention(q, k, v, lam), subln_g) * (1.0 - lam_init)
    y_diff = o_d.reshape(B, S, -1) @ w_bd

    q_g = gq.reshape(B, S, GLA_HEADS, GLA_DK) * (GLA_DK ** -0.5)
    k_g = gk.reshape(B, S, GLA_HEADS, GLA_DK)
    v_g = gv.reshape(B, S, GLA_HEADS, GLA_DV)
    log_af = (jax.nn.log_sigmoid((glf @ w2f + bf).astype(jnp.float32)) / GLA_TAU).reshape(B, S, GLA_HEADS, GLA_DK)
    log_ab = (jax.nn.log_sigmoid((glb @ w2b + bb).astype(jnp.float32)) / GLA_TAU).reshape(B, S, GLA_HEADS, GLA_DK)
    o_f = gla_scan(q_g, k_g, v_g, log_af)
    flip = lambda t: jnp.flip(t, axis=1)
    o_b = flip(gla_scan(flip(q_g), flip(k_g), flip(v_g), flip(log_ab)))
    o_g = rms_norm(o_f + o_b, gla_g) * jax.nn.silu(gr.reshape(B, S, GLA_HEADS, GLA_DV))
    y_gla = o_g.reshape(B, S, -1) @ w_bg

    merged = jax.nn.sigmoid(gate_a) * y_diff + jax.nn.sigmoid(gate_b) * y_gla
    return merged @ w_o


def conv_ffn(h, w_gate, w_up, conv_w, conv_b, w_down):
    g = h @ w_gate
    g = lax.conv_general_dilated(
        g, conv_w[:, None, :], window_strides=(1,),
        padding=((CONV_WIDTH // 2, CONV_WIDTH // 2),),
        dimension_numbers=('NWC', 'WIO', 'NWC'),
        feature_group_count=g.shape[-1]) + conv_b
    return (jax.nn.silu(g) * (h @ w_up)) @ w_down


def setup_inputs(seed: int = 0) -> dict:
    key = jax.random.key(seed)
    ks = jax.random.split(key, 27)
    f32 = jnp.float32
    L, D, F = DEPTH, D_MODEL, D_FF

    def nrm(k, shape, scale):
        return jax.random.normal(k, shape, f32) * scale

    x = nrm(ks[0], (BATCH, SEQ, D), 1.0)
    c = nrm(ks[1], (BATCH, D), 1.0)
    positions = (jnp.arange(SEQ, dtype=jnp.int32)[None, :]
                 + jax.random.randint(ks[2], (BATCH, 1), 0, MAX_POS_OFFSET, dtype=jnp.int32))
    return {
        "x": x,
        "c": c,
        "positions": positions,
        "w_ada": nrm(ks[3], (L, D, 6 * D), D ** -0.5),
        "b_ada": nrm(ks[4], (L, 6 * D), 0.02),
        "norm_mix_g": 1.0 + nrm(ks[5], (L, D), 0.02),
        "w_in": nrm(ks[6], (L, D, IN_COLS), D ** -0.5),
        "lambda_q1": nrm(ks[7], (L, DIFF_DQK), 0.1),
        "lambda_k1": nrm(ks[8], (L, DIFF_DQK), 0.1),
        "lambda_q2": nrm(ks[9], (L, DIFF_DQK), 0.1),
        "lambda_k2": nrm(ks[10], (L, DIFF_DQK), 0.1),
        "diff_subln_g": 1.0 + nrm(ks[11], (L, DIFF_DV), 0.02),
        "gla_w2_fwd": nrm(ks[12], (L, GLA_GATE_RANK, GLA_HEADS * GLA_DK), GLA_GATE_RANK ** -0.5),
        "gla_b_fwd": nrm(ks[13], (L, GLA_HEADS * GLA_DK), 0.02),
        "gla_w2_bwd": nrm(ks[14], (L, GLA_GATE_RANK, GLA_HEADS * GLA_DK), GLA_GATE_RANK ** -0.5),
        "gla_b_bwd": nrm(ks[15], (L, GLA_HEADS * GLA_DK), 0.02),
        "gla_norm_g": 1.0 + nrm(ks[16], (L, GLA_DV), 0.02),
        "w_branch_diff": nrm(ks[17], (L, DIFF_HEADS * DIFF_DV, D), (DIFF_HEADS * DIFF_DV) ** -0.5),
        "w_branch_gla": nrm(ks[18], (L, GLA_HEADS * GLA_DV, D), (GLA_HEADS * GLA_DV) ** -0.5),
        "w_out": nrm(ks[19], (L, D, D), D ** -0.5),
        "norm_ffn_g": 1.0 + nrm(ks[20], (L, D), 0.02),
        "w_gate": nrm(ks[21], (L, D, F), D ** -0.5),
        "w_up": nrm(ks[22], (L, D, F), D ** -0.5),
        "conv_w": nrm(ks[23], (L, CONV_WIDTH, F), CONV_WIDTH ** -0.5),
        "conv_b": nrm(ks[24], (L, F), 0.02),
        "w_down": nrm(ks[25], (L, F, D), F ** -0.5),
        "final_norm_g": 1.0 + nrm(ks[26], (D,), 0.02),
    }


def reference(x, c, positions, w_ada, b_ada, norm_mix_g, w_in, lambda_q1, lambda_k1,
              lambda_q2, lambda_k2, diff_subln_g, gla_w2_fwd, gla_b_fwd, gla_w2_bwd,
              gla_b_bwd, gla_norm_g, w_branch_diff, w_branch_gla, w_out, norm_ffn_g,
              w_gate, w_up, conv_w, conv_b, w_down, final_norm_g):
    cos, sin = rotary_tables(positions)
    c_act = jax.nn.silu(c)
    for l in range(DEPTH):
        mod = c_act @ w_ada[l] + b_ada[l]
        sh1, sc1, gt1, sh2, sc2, gt2 = [m[:, None, :] for m in jnp.split(mod, 6, axis=-1)]
        h = rms_norm(x, norm_mix_g[l]) * (1.0 + sc1) + sh1
        x = x + gt1 * token_mixer(
            h, cos, sin, l, w_in[l], lambda_q1[l], lambda_k1[l], lambda_q2[l], lambda_k2[l],
            diff_subln_g[l], gla_w2_fwd[l], gla_b_fwd[l], gla_w2_bwd[l], gla_b_bwd[l],
            gla_norm_g[l], w_branch_diff[l], w_branch_gla[l], w_out[l])
        h = rms_norm(x, norm_ffn_g[l]) * (1.0 + sc2) + sh2
        x = x + gt2 * conv_ffn(h, w_gate[l], w_up[l], conv_w[l], conv_b[l], w_down[l])
    return rms_norm(x, final_norm_g)
```

```python
import contextlib
import math
import numpy as np
import concourse.bass as bass
import concourse.mybir as mybir
from concourse.bass_utils import run_bass_kernel_spmd

F32 = mybir.dt.float32
BF16 = mybir.dt.bfloat16
I32 = mybir.dt.int32
AF = mybir.ActivationFunctionType
ALU = mybir.AluOpType
AX = mybir.AxisListType

D = 2048
S = 4096
L = 2
FF = 5632
NT = S // 128
KC = D // 128
NFC = FF // 128
IN_COLS = 10272
EPS = 1e-6
SEM_LIMIT = 30000
SYNC_SAME = True
TWO_PI = 2.0 * math.pi

C_ID, C_TRIF, C_TRIB, C_MF, C_MB, C_OL, C_OF, C_INVF = 0, 128, 256, 384, 512, 640, 641, 642
NCONST = 658


class Buf:
    __slots__ = ("w", "r")

    def __init__(self):
        self.w = None
        self.r = []


class Prog:
    COMPUTE = ("pe", "act", "dve", "pool")

    def __init__(self, nc, sync_same_engine=SYNC_SAME, n_dma_sems=16):
        self.nc = nc
        self.sync_same = sync_same_engine
        self.queues = {"pe": [], "act": [], "dve": [], "pool": [], "sp": []}
        self._ctx = []
        self.sems = {}
        self.cur = {}
        self.waited = {q: {} for q in self.queues}
        for e in self.COMPUTE:
            self._new_epoch(e)
        self.dma_ring = {}
        for q in ("sp", "pool", "act"):
            ring = [[self._alloc_sem(f"dma_{q}_{i}"), 0] for i in range(n_dma_sems)]
            self.dma_ring[q] = [ring, 0]
        self.n_instr = 0

    def _alloc_sem(self, name):
        cm = self.nc.semaphore(name)
        h = cm.__enter__()
        self._ctx.append(cm)
        self.sems[name] = h
        return name

    def _new_epoch(self, e):
        ep = 0 if e not in self.cur else self.cur[e][2] + 1
        self.cur[e] = [self._alloc_sem(f"s_{e}_{ep}"), 0, ep]

    def _waits(self, q, reads, writes):
        need = {}

        def add(ev):
            if ev is not None and need.get(ev[0], 0) < ev[1]:
                need[ev[0]] = ev[1]
        for b in reads:
            add(b.w)
        for b in writes:
            add(b.w)
            for ev in b.r:
                add(ev)
        out = []
        wq = self.waited[q]
        for k, v in need.items():
            if q in self.COMPUTE and k == self.cur[q][0]:
                if not self.sync_same or v > self.cur[q][1]:
                    continue
            if wq.get(k, 0) >= v:
                continue
            wq[k] = v
            out.append((k, v))
        return out

    def op(self, eng, fn, reads=(), writes=(), inc=True):
        waits = self._waits(eng, reads, writes)
        cur = self.cur[eng]
        if inc:
            cur[1] += 1
            ev = (cur[0], cur[1])
        else:
            ev = (cur[0], cur[1] + 1)
        for b in reads:
            b.r.append(ev)
            if len(b.r) > 64:
                b.r = b.r[-64:] if False else self._compact(b.r)
        for b in writes:
            b.w = ev
            b.r = []
        self.queues[eng].append((waits, fn, (cur[0], 1) if inc else None))
        self.n_instr += 1
        if inc and cur[1] >= SEM_LIMIT:
            self._new_epoch(eng)
        return ev

    @staticmethod
    def _compact(evs):
        m = {}
        for k, v in evs:
            if m.get(k, 0) < v:
                m[k] = v
        return list(m.items())

    def dma(self, q, fn, reads=(), writes=()):
        ring, idx = self.dma_ring[q]
        slot = ring[idx % len(ring)]
        self.dma_ring[q][1] = idx + 1
        waits = self._waits(q, reads, writes)
        if slot[1] > 0 and self.waited[q].get(slot[0], 0) < slot[1]:
            self.waited[q][slot[0]] = slot[1]
            waits.append((slot[0], slot[1]))
        slot[1] += 16
        ev = (slot[0], slot[1])
        for b in reads:
            b.r.append(ev)
            if len(b.r) > 64:
                b.r = self._compact(b.r)
        for b in writes:
            b.w = ev
            b.r = []
        self.queues[q].append((waits, fn, (slot[0], 16)))
        self.n_instr += 1
        return ev

    def barrier(self):
        evs = []
        for e in self.COMPUTE:
            k, c, _ = self.cur[e]
            if c > 0:
                evs.append((k, c))
        for q in self.dma_ring:
            for slot in self.dma_ring[q][0]:
                if slot[1] > 0:
                    evs.append((slot[0], slot[1]))
        for q in self.queues:
            waits = []
            for k, v in evs:
                if self.waited[q].get(k, 0) < v:
                    self.waited[q][k] = v
                    waits.append((k, v))
            if waits:
                self.queues[q].append((waits, None, None))

    def emit(self):
        nc, sems, queues = self.nc, self.sems, self.queues

        def run(engine, lst):
            for waits, fn, inc in lst:
                for k, v in waits:
                    engine.wait_ge(sems[k], v)
                if fn is None:
                    continue
                ins = fn(engine)
                if inc is not None:
                    ins.then_inc(sems[inc[0]], inc[1])

        with nc.Block() as block:
            @block.sync
            def _(sync):
                run(sync, queues["sp"])

            @block.tensor
            def _(tensor):
                run(tensor, queues["pe"])

            @block.scalar
            def _(scalar):
                run(scalar, queues["act"])

            @block.vector
            def _(vector):
                run(vector, queues["dve"])

            @block.gpsimd
            def _(gpsimd):
                run(gpsimd, queues["pool"])
        for q in queues:
            queues[q] = []


class Ctx:
    _n = 0

    def __init__(self, nc):
        self.nc = nc
        self.es = contextlib.ExitStack()

    def sb(self, shape, dt, name="t"):
        Ctx._n += 1
        return self.es.enter_context(self.nc.sbuf_tensor(f"{name}_{Ctx._n}", list(shape), dt))

    def ps(self, shape, dt, name="p"):
        Ctx._n += 1
        return self.es.enter_context(self.nc.psum_tensor(f"{name}_{Ctx._n}", list(shape), dt))

    def close(self):
        self.es.close()


def build(debug=False, stop=None, nlayers=L, skip=(), only=None, scratch_in=(), scratch_out=None):
    nc = bass.Bass("TRN2", target_bir_lowering=False)
    P = Prog(nc)
    skind = "ExternalOutput" if debug else "Internal"

    def din(name, shape, dt=F32):
        return nc.dram_tensor(name, list(shape), dt, kind="ExternalInput").ap()

    def dscr(name, shape, dt):
        if name in scratch_in:
            kind = "ExternalInput"
            used_scratch_in.append(name)
        elif debug and (scratch_out is None or name in scratch_out):
            kind = "ExternalOutput"
        else:
            kind = "Internal"
        return nc.dram_tensor(name, list(shape), dt, kind=kind).ap()

    used_scratch_in = []

    class _Lazy:
        def __init__(self, name, shape, dt=F32):
            self.name, self.shape, self.dt, self._ap = name, shape, dt, None

        def _get(self):
            if self._ap is None:
                self._ap = din(self.name, self.shape, self.dt)
                used_inputs.append(self.name)
            return self._ap

        def __getitem__(self, k):
            return self._get()[k]

        def rearrange(self, *a, **k):
            return self._get().rearrange(*a, **k)

        def partition_broadcast(self, n):
            return self._get().partition_broadcast(n)

    used_inputs = []
    x_in = _Lazy("x", [S, D])
    c_in = _Lazy("c", [128, 16])
    pos_in = _Lazy("pos", [128, NT], I32)
    consts_in = _Lazy("consts", [128, NCONST])
    w_ada = _Lazy("w_ada", [L, D, 6 * D])
    b_ada = _Lazy("b_ada", [L, 6 * D])
    norm_mix_g = _Lazy("norm_mix_g", [L, D])
    w_in = _Lazy("w_in", [L, D, IN_COLS])
    lq1 = _Lazy("lambda_q1", [L, 128])
    lk1 = _Lazy("lambda_k1", [L, 128])
    lq2 = _Lazy("lambda_q2", [L, 128])
    lk2 = _Lazy("lambda_k2", [L, 128])
    subln_g = _Lazy("diff_subln_g", [L, 256])
    w2f = _Lazy("gla_w2_fwd", [L, 16, 512])
    bfw = _Lazy("gla_b_fwd", [L, 512])
    w2b = _Lazy("gla_w2_bwd", [L, 16, 512])
    bbw = _Lazy("gla_b_bwd", [L, 512])
    gla_g = _Lazy("gla_norm_g", [L, 256])
    w_bd = _Lazy("w_branch_diff", [L, 1024, D])
    w_bg = _Lazy("w_branch_gla", [L, 1024, D])
    w_out = _Lazy("w_out", [L, D, D])
    norm_ffn_g = _Lazy("norm_ffn_g", [L, D])
    w_gate = _Lazy("w_gate", [L, D, FF])
    w_up = _Lazy("w_up", [L, D, FF])
    conv_w = _Lazy("conv_w", [L, 3, FF])
    conv_b = _Lazy("conv_b", [L, FF])
    w_down = _Lazy("w_down", [L, FF, D])
    fin_g = _Lazy("final_norm_g", [D])
    out = nc.dram_tensor("out", [S, D], F32, kind="ExternalOutput").ap()

    modrows = dscr("modrows", [L, 6, D], F32)
    qT = dscr("qT", [8, 128, S], BF16); kT = dscr("kT", [8, 128, S], BF16)
    vd = dscr("vd", [S, 1024], BF16)
    gq = dscr("gq", [S, 512], F32); gk = dscr("gk", [S, 512], F32)
    gv = dscr("gv", [S, 1024], BF16); gr = dscr("gr", [S, 1024], BF16)
    glT = dscr("glT", [32, S], F32)
    gabT = dscr("gabT", [32, 128, S], BF16)
    odT = dscr("odT", [8, 128, S], BF16)
    of_ = dscr("of", [S, 1024], F32)
    ogT = dscr("ogT", [8, 128, S], BF16)
    mT = dscr("mT", [16, 128, S], BF16)
    h2T = dscr("h2T", [16, 128, S], BF16)
    aT = dscr("aT", [NFC, 128, S], BF16)
    xa = dscr("xa", [S, D], F32)
    xb = dscr("xb", [S, D], F32)

    G = Ctx(nc)
    consts = G.sb([128, NCONST], F32, "consts"); b_consts = Buf()
    ident_bf = G.sb([128, 128], BF16, "identbf"); b_identbf = Buf()
    cosT = G.sb([128, NT, 16], F32, "cos"); sinT = G.sb([128, NT, 16], F32, "sin"); b_cs = Buf()
    ident = consts[:, C_ID:C_ID + 128]

    consts_in._get()
    P.dma("sp", lambda e: e.dma_start(out=consts[:], in_=consts_in[:, :]), writes=[b_consts])
    P.op("dve", lambda e: e.tensor_copy(ident_bf[:], ident), reads=[b_consts], writes=[b_identbf])

    def phase_rotary():
        pos_in._get()
        X = Ctx(nc)
        posi = X.sb([128, NT], I32); posf = X.sb([128, NT], F32); ang = X.sb([128, NT, 16], F32)
        r = X.sb([128, NT, 16], F32); b_t = Buf()
        P.dma("sp", lambda e: e.dma_start(out=posi[:], in_=pos_in[:, :]), writes=[b_t])
        P.op("dve", lambda e: e.tensor_copy(posf[:], posi[:]), reads=[b_t], writes=[b_t])
        invf = consts[:, C_INVF:C_INVF + 16]
        P.op("dve", lambda e: e.tensor_tensor(ang[:], posf[:].unsqueeze(2).to_broadcast([128, NT, 16]),
                                              invf.unsqueeze(1).to_broadcast([128, NT, 16]), ALU.mult),
             reads=[b_t, b_consts], writes=[b_t])
        ki = X.sb([128, NT, 16], I32); kf = X.sb([128, NT, 16], F32); mk = X.sb([128, NT, 16], F32)

        def sin_of(dst, shift):
            P.op("dve", lambda e: e.tensor_scalar_add(r[:], ang[:], shift), reads=[b_t, b_cs], writes=[b_t])
            P.op("dve", lambda e: e.tensor_scalar_mul(kf[:], r[:], 1.0 / TWO_PI), reads=[b_t], writes=[b_t])
            P.op("dve", lambda e: e.tensor_copy(ki[:], kf[:]), reads=[b_t], writes=[b_t])
            P.op("dve", lambda e: e.tensor_copy(kf[:], ki[:]), reads=[b_t], writes=[b_t])
            P.op("dve", lambda e: e.scalar_tensor_tensor(r[:], kf[:], -TWO_PI, r[:], ALU.mult, ALU.add), reads=[b_t], writes=[b_t])
            P.op("dve", lambda e: e.tensor_single_scalar(mk[:], r[:], math.pi, ALU.is_gt), reads=[b_t], writes=[b_t])
            P.op("dve", lambda e: e.scalar_tensor_tensor(r[:], mk[:], -TWO_PI, r[:], ALU.mult, ALU.add), reads=[b_t], writes=[b_t])
            P.op("dve", lambda e: e.tensor_scalar(r[:], r[:], math.pi, -math.pi, ALU.min, ALU.max), reads=[b_t], writes=[b_t])
            P.op("act", lambda e: e.activation(dst[:], r[:], AF.Sin), reads=[b_t], writes=[b_cs])
        sin_of(sinT, 0.0)
        sin_of(cosT, math.pi / 2)
        P.barrier(); P.emit(); X.close()

    def phase_adaln():
        [t._get() for t in (c_in, w_ada, b_ada, norm_mix_g, norm_ffn_g)]
        X = Ctx(nc)
        cact = X.sb([128, 16], F32); b_c = Buf()
        wa = [X.sb([128, 16, 512], BF16, "wa") for _ in range(3)]; b_wa = [Buf() for _ in range(3)]
        cbf = X.sb([128, 16], BF16, "cbf")
        row = X.sb([1, 6 * D], F32, "row"); b_row = Buf()
        brow = X.sb([1, 6 * D], F32, "brow"); b_brow = Buf()
        grow = X.sb([1, 2 * D], F32, "grow"); b_grow = Buf()
        pacc = [X.ps([1, 512], F32, "pacc") for _ in range(2)]; b_pacc = [Buf(), Buf()]
        P.dma("sp", lambda e: e.dma_start(out=cact[:], in_=c_in[:, :]), writes=[b_c])
        P.op("act", lambda e: e.activation(cact[:], cact[:], AF.Silu), reads=[b_c], writes=[b_c])
        P.op("dve", lambda e: e.tensor_copy(cbf[:], cact[:]), reads=[b_c], writes=[b_c])
        it = 0
        for l in range(L):
            P.dma("sp", lambda e, l=l: e.dma_start(out=brow[:], in_=b_ada[l:l + 1, :]), writes=[b_brow])
            P.dma("sp", lambda e, l=l: e.dma_start(out=grow[:, 0:D], in_=norm_mix_g[l:l + 1, :]), writes=[b_grow])
            P.dma("sp", lambda e, l=l: e.dma_start(out=grow[:, D:2 * D], in_=norm_ffn_g[l:l + 1, :]), writes=[b_grow])
            for nb in range(24):
                i = it % 3; pi = it % 2; it += 1
                src = w_ada[l][:, nb * 512:(nb + 1) * 512].rearrange("(p j) n -> p j n", j=16)
                P.dma("pool", lambda e, i=i, src=src: e.dma_start(out=wa[i][:], in_=src), writes=[b_wa[i]])
                for j in range(16):
                    P.op("pe", lambda e, i=i, pi=pi, j=j: e.matmul(pacc[pi][:], cbf[:, j:j + 1], wa[i][:, j, :],
                                                                    start=(j == 0), stop=(j == 15)),
                         reads=[b_c, b_wa[i]], writes=[b_pacc[pi]], inc=(j == 15))
                P.op("dve", lambda e, pi=pi, nb=nb: e.tensor_add(row[:, nb * 512:(nb + 1) * 512], pacc[pi][:],
                                                                brow[:, nb * 512:(nb + 1) * 512]),
                     reads=[b_pacc[pi], b_brow], writes=[b_row])
            P.op("dve", lambda e: e.scalar_tensor_tensor(row[:, D:2 * D], row[:, D:2 * D], 1.0, grow[:, 0:D], ALU.add, ALU.mult),
                 reads=[b_row, b_grow], writes=[b_row])
            P.op("dve", lambda e: e.scalar_tensor_tensor(row[:, 4 * D:5 * D], row[:, 4 * D:5 * D], 1.0, grow[:, D:2 * D], ALU.add, ALU.mult),
                 reads=[b_row, b_grow], writes=[b_row])
            for dst, srcc in enumerate([1, 0, 2, 4, 3, 5]):
                P.dma("sp", lambda e, l=l, dst=dst, srcc=srcc: e.dma_start(out=modrows[l, dst:dst + 1, :], in_=row[:, srcc * D:(srcc + 1) * D]),
                      reads=[b_row])
        P.barrier(); P.emit(); X.close()

    def bcast_load(q, tile, b_tile, row_ap):
        P.dma(q, lambda e: e.dma_start(out=tile, in_=row_ap.partition_broadcast(128)), writes=[b_tile])

    def phase_inproj(l, xsrc):
        [t._get() for t in (x_in, w_in)]
        X = Ctx(nc)
        hT = X.sb([128, KC, S], BF16, "hT"); b_hT = [Buf() for _ in range(NT)]
        Y = Ctx(nc)
        Ab = Y.sb([128, D], F32, "Ab"); Bb = Y.sb([128, D], F32, "Bb"); b_ab = Buf()
        bcast_load("sp", Ab[:], b_ab, modrows[l, 0])
        bcast_load("sp", Bb[:], b_ab, modrows[l, 1])
        xt = [Y.sb([128, D], F32, "xt") for _ in range(2)]; b_xt = [Buf(), Buf()]
        hbf = [Y.sb([128, D], BF16, "hbf") for _ in range(2)]; b_hbf = [Buf(), Buf()]
        junk = Y.sb([128, D], BF16, "junk"); ss = [Y.sb([128, 4], F32, "ss") for _ in range(2)]; b_tmp = [Buf(), Buf()]
        tp = [Y.ps([128, KC, 128], BF16, "tp") for _ in range(2)]; b_tp = [Buf(), Buf()]
        pend1 = []
        for t in range(NT):
            i = t % 2
            P.dma("sp", lambda e, i=i, t=t: e.dma_start(out=xt[i][:], in_=xsrc[t * 128:(t + 1) * 128, :]), writes=[b_xt[i]])
            rms_mod(xt[i], b_xt[i], Ab, Bb, b_ab, hbf[i], b_hbf[i], ss[i], junk, b_tmp[i])

            def later(i=i, t=t):
                for kc in range(KC):
                    P.op("pe", lambda e, kc=kc: e.transpose(tp[i][:, kc, :], hbf[i][:, kc * 128:(kc + 1) * 128], ident_bf[:]),
                         reads=[b_hbf[i], b_identbf], writes=[b_tp[i]], inc=(kc == KC - 1))
                P.op("act", lambda e: e.copy(hT[:, :, t * 128:(t + 1) * 128], tp[i][:]), reads=[b_tp[i]], writes=[b_hT[t]])
            pend1.append(later)
            if len(pend1) > 1:
                pend1.pop(0)()
        for f in pend1:
            f()
        P.barrier(); P.emit(); Y.close()
        if stop == f"inproj{l}a":
            X.close()
            return
        Y = Ctx(nc)
        wt = [Y.sb([128, KC, 512], BF16, "wt") for _ in range(2)]; b_wt = [[Buf() for _ in range(KC)] for _ in range(2)]
        acc = [Y.ps([128, 512], F32, "acc") for _ in range(3)]; b_acc = [Buf() for _ in range(3)]
        tq = [Y.ps([128, 4, 128], BF16, "tq") for _ in range(2)]; b_tq = [Buf(), Buf()]
        stg_bf = [Y.sb([128, 512], BF16, "stgb") for _ in range(3)]; b_sbf = [Buf() for _ in range(3)]
        stg_f = [Y.sb([128, 512], F32, "stgf") for _ in range(3)]; b_sf = [Buf() for _ in range(3)]
        stg_T = [Y.sb([128, 4, 128], BF16, "stgT") for _ in range(2)]; b_sT = [Buf(), Buf()]
        rt = [Y.sb([128, 4, 4, 16], F32, "rt") for _ in range(2)]; b_rt = [Buf(), Buf()]
        wsm = Y.sb([128, KC, 32], BF16, "wsm"); b_wsm = [Buf() for _ in range(KC)]
        cnt = {"acc": 0, "sbf": 0, "sf": 0, "tq": 0, "w": 0}

        def load_w(c0, n, tile, b):
            for kc in range(KC):
                src = w_in[l][kc * 128:(kc + 1) * 128, c0:c0 + n]
                P.dma("pool", lambda e, kc=kc, src=src: e.dma_start(out=tile[:, kc, :], in_=src), writes=[b[kc]])

        def tok_major(c0, handler, defer=0):
            wi = cnt["w"] % 2; cnt["w"] += 1
            load_w(c0, 512, wt[wi][:], b_wt[wi])
            pend = []
            for t in range(NT):
                ai = cnt["acc"] % 3; cnt["acc"] += 1
                for kc in range(KC):
                    P.op("pe", lambda e, ai=ai, kc=kc, t=t, wi=wi: e.matmul(acc[ai][:], hT[:, kc, t * 128:(t + 1) * 128], wt[wi][:, kc, :],
                                                                         start=(kc == 0), stop=(kc == KC - 1)),
                         reads=[b_hT[t], b_wt[wi][kc]], writes=[b_acc[ai]], inc=(kc == KC - 1))
                later = handler(t, acc[ai], b_acc[ai])
                if later is not None:
                    pend.append(later)
                    if len(pend) > defer:
                        pend.pop(0)()
            for f in pend:
                f()

        def h_store(dst, col0, f32=False, func=None):
            def h(t, a, b_a):
                if f32:
                    si = cnt["sf"] % 3; cnt["sf"] += 1
                    st, bs = stg_f[si], b_sf[si]
                else:
                    si = cnt["sbf"] % 3; cnt["sbf"] += 1
                    st, bs = stg_bf[si], b_sbf[si]
                if func is None:
                    P.op("act", lambda e: e.copy(st[:], a[:]), reads=[b_a], writes=[bs])
                else:
                    P.op("act", lambda e: e.activation(st[:], a[:], func), reads=[b_a], writes=[bs])
                P.dma("sp", lambda e: e.dma_start(out=dst[t * 128:(t + 1) * 128, col0:col0 + 512], in_=st[:]), reads=[bs])
            return h

        def h_rot(dstT, hm0):
            def h(t, a, b_a):
                si = cnt["sbf"] % 3; cnt["sbf"] += 1
                st, bs = stg_bf[si], b_sbf[si]
                ri = cnt["tq"] % 2; cnt["tq"] += 1
                fi = cnt["sf"] % 3; cnt["sf"] += 1
                s32, b32 = stg_f[fi], b_sf[fi]
                P.op("act", lambda e: e.copy(s32[:], a[:]), reads=[b_a], writes=[b32])
                R = rt[ri]
                s4 = s32[:].rearrange("p (h d) -> p h d", h=4)
                x1 = s4[:, :, 0:16]; x2 = s4[:, :, 16:32]
                cb = cosT[:, t, :].unsqueeze(1).to_broadcast([128, 4, 16])
                sn = sinT[:, t, :].unsqueeze(1).to_broadcast([128, 4, 16])
                P.op("dve", lambda e: e.tensor_tensor(R[:, 0], x1, cb, ALU.mult), reads=[b32, b_cs], writes=[b_rt[ri]])
                P.op("dve", lambda e: e.tensor_tensor(R[:, 1], x2, sn, ALU.mult), reads=[b32, b_cs], writes=[b_rt[ri]])
                P.op("dve", lambda e: e.tensor_tensor(R[:, 2], x2, cb, ALU.mult), reads=[b32, b_cs], writes=[b_rt[ri]])
                P.op("dve", lambda e: e.tensor_tensor(R[:, 3], x1, sn, ALU.mult), reads=[b32, b_cs], writes=[b_rt[ri]])
                P.op("dve", lambda e: e.tensor_tensor(x1, R[:, 0], R[:, 1], ALU.subtract), reads=[b_rt[ri]], writes=[b32])
                P.op("dve", lambda e: e.tensor_tensor(x2, R[:, 2], R[:, 3], ALU.add), reads=[b_rt[ri]], writes=[b32])
                P.op("dve", lambda e: e.tensor_copy(st[:], s32[:]), reads=[b32], writes=[bs])

                def later():
                    for hh in range(4):
                        P.op("pe", lambda e, hh=hh: e.transpose(tq[ri][:, hh, :], st[:, hh * 128:(hh + 1) * 128], ident_bf[:]),
                             reads=[bs, b_identbf], writes=[b_tq[ri]], inc=(hh == 3))
                    P.op("act", lambda e: e.copy(stg_T[ri][:], tq[ri][:]), reads=[b_tq[ri]], writes=[b_sT[ri]])
                    P.dma("sp", lambda e: e.dma_start(out=dstT[hm0:hm0 + 4, :, t * 128:(t + 1) * 128].rearrange("h p n -> p h n"), in_=stg_T[ri][:]),
                          reads=[b_sT[ri]])
                return later
            return h

        tok_major(0, h_rot(qT, 0), defer=1)
        if stop == f"inproj{l}b":
            P.barrier(); P.emit(); Y.close(); X.close()
            return
        tok_major(512, h_rot(qT, 4), defer=1)
        tok_major(1024, h_rot(kT, 0), defer=1); tok_major(1536, h_rot(kT, 4), defer=1)
        tok_major(2048, h_store(vd, 0)); tok_major(2560, h_store(vd, 512))
        tok_major(3072, h_store(gq, 0, True)); tok_major(3584, h_store(gk, 0, True))
        tok_major(4096, h_store(gv, 0)); tok_major(4608, h_store(gv, 512))
        tok_major(5120, h_store(gr, 0, func=AF.Silu)); tok_major(5632, h_store(gr, 512, func=AF.Silu))
        load_w(6144, 32, wsm[:], b_wsm)
        for tb in range(8):
            ai = cnt["acc"] % 3; cnt["acc"] += 1
            for kc in range(KC):
                P.op("pe", lambda e, ai=ai, kc=kc, tb=tb: e.matmul(acc[ai][0:32, :], wsm[:, kc, :], hT[:, kc, tb * 512:(tb + 1) * 512],
                                                                  start=(kc == 0), stop=(kc == KC - 1)),
                     reads=[b_wsm[kc]] + b_hT[tb * 4:(tb + 1) * 4], writes=[b_acc[ai]], inc=(kc == KC - 1))
            si = cnt["sf"] % 3; cnt["sf"] += 1
            P.op("act", lambda e, si=si, ai=ai: e.copy(stg_f[si][0:32, :], acc[ai][0:32, :]), reads=[b_acc[ai]], writes=[b_sf[si]])
            P.dma("sp", lambda e, si=si, tb=tb: e.dma_start(out=glT[:, tb * 512:(tb + 1) * 512], in_=stg_f[si][0:32, :]), reads=[b_sf[si]])
        for gg in range(8):
            wi = cnt["w"] % 2; cnt["w"] += 1
            load_w(6176 + gg * 512, 512, wt[wi][:], b_wt[wi])
            for j in range(4):
                ch = gg * 4 + j
                for tb in range(8):
                    ai = cnt["acc"] % 3; cnt["acc"] += 1
                    for kc in range(KC):
                        P.op("pe", lambda e, ai=ai, kc=kc, tb=tb, wi=wi, j=j: e.matmul(acc[ai][:], wt[wi][:, kc, j * 128:(j + 1) * 128],
                                                                                     hT[:, kc, tb * 512:(tb + 1) * 512],
                                                                                     start=(kc == 0), stop=(kc == KC - 1)),
                             reads=[b_wt[wi][kc]] + b_hT[tb * 4:(tb + 1) * 4], writes=[b_acc[ai]], inc=(kc == KC - 1))
                    si = cnt["sbf"] % 3; cnt["sbf"] += 1
                    eng = "act" if (tb % 2 == 0) else "dve"
                    if eng == "act":
                        P.op("act", lambda e, si=si, ai=ai: e.copy(stg_bf[si][:], acc[ai][:]), reads=[b_acc[ai]], writes=[b_sbf[si]])
                    else:
                        P.op("dve", lambda e, si=si, ai=ai: e.tensor_copy(stg_bf[si][:], acc[ai][:]), reads=[b_acc[ai]], writes=[b_sbf[si]])
                    P.dma("sp", lambda e, si=si, ch=ch, tb=tb: e.dma_start(out=gabT[ch, :, tb * 512:(tb + 1) * 512], in_=stg_bf[si][:]),
                          reads=[b_sbf[si]])
        P.barrier(); P.emit(); Y.close(); X.close()

    def rms_mod(xt, b_xt, Ab, Bb, b_ab, hbf, b_hbf, ss, junk, b_tmp, dwidth=D):
        P.op("act", lambda e: e.activation(junk[:], xt[:], AF.Square, accum_out=ss[:, 0:1]), reads=[b_xt], writes=[b_tmp])
        P.op("dve", lambda e: e.tensor_scalar(ss[:, 1:2], ss[:, 0:1], 1.0 / dwidth, EPS, ALU.mult, ALU.add), reads=[b_tmp], writes=[b_tmp])
        P.op("act", lambda e: e.activation(ss[:, 1:2], ss[:, 1:2], AF.Sqrt), reads=[b_tmp], writes=[b_tmp])
        P.op("dve", lambda e: e.reciprocal(ss[:, 2:3], ss[:, 1:2]), reads=[b_tmp], writes=[b_tmp])
        P.op("dve", lambda e: e.scalar_tensor_tensor(xt[:], xt[:], ss[:, 2:3], Ab[:], ALU.mult, ALU.mult),
             reads=[b_xt, b_tmp, b_ab], writes=[b_xt])
        if Bb is not None:
            P.op("dve", lambda e: e.tensor_add(hbf[:], xt[:], Bb[:]), reads=[b_xt, b_ab], writes=[b_hbf])

    def phase_diffattn(l):
        [t._get() for t in (lq1, lk1, lq2, lk2, subln_g)]
        X = Ctx(nc)
        lam_init = 0.8 - 0.6 * math.exp(-0.3 * l)
        scale = 128 ** -0.5
        lt = X.sb([128, 4, 128], F32, "lt"); b_lt = Buf()
        lv = X.sb([128, 8], F32, "lv"); b_lv = Buf()
        for i, src in enumerate([lq1, lk1, lq2, lk2]):
            bcast_load("sp", lt[:, i, :], b_lt, src[l])
        P.op("dve", lambda e: e.tensor_tensor(lt[:, 0, :], lt[:, 0, :], lt[:, 1, :], ALU.mult), reads=[b_lt], writes=[b_lt])
        P.op("dve", lambda e: e.tensor_tensor(lt[:, 2, :], lt[:, 2, :], lt[:, 3, :], ALU.mult), reads=[b_lt], writes=[b_lt])
        P.op("dve", lambda e: e.reduce_sum(lv[:, 0:1], lt[:, 0, :], AX.X), reads=[b_lt], writes=[b_lv])
        P.op("dve", lambda e: e.reduce_sum(lv[:, 1:2], lt[:, 2, :], AX.X), reads=[b_lt], writes=[b_lv])
        P.op("act", lambda e: e.activation(lv[:, 2:4], lv[:, 0:2], AF.Exp), reads=[b_lv], writes=[b_lv])
        P.op("dve", lambda e: e.tensor_tensor(lv[:, 4:5], lv[:, 3:4], lv[:, 2:3], ALU.subtract), reads=[b_lv], writes=[b_lv])
        P.op("dve", lambda e: e.tensor_scalar_add(lv[:, 5:6], lv[:, 4:5], -lam_init), reads=[b_lv], writes=[b_lv])
        nlam = lv[:, 5:6]
        sg = X.sb([128, 256], F32, "sg"); b_sg = Buf()
        bcast_load("sp", sg[:], b_sg, subln_g[l])
        P.op("dve", lambda e: e.tensor_scalar_mul(sg[:], sg[:], 1.0 - lam_init), reads=[b_sg], writes=[b_sg])

        vaug = [X.sb([128, NT, 257], BF16, "vaug") for _ in range(2)]; b_v = [Buf(), Buf()]
        kt = [X.sb([128, S], BF16, "kt") for _ in range(4)]; b_kt = [Buf() for _ in range(4)]
        qb_t = [X.sb([128, 512], BF16, "qb") for _ in range(4)]; b_qb = [Buf() for _ in range(4)]
        pT = [X.sb([128, 512], BF16, "pT") for _ in range(3)]; b_pT = [Buf() for _ in range(3)]
        osb = [X.sb([128, 4, 257], F32, "osb") for _ in range(2)]; b_osb = [Buf(), Buf()]
        o1 = X.sb([128, 4, 256], F32, "o1"); b_o1 = Buf()
        od4 = X.sb([128, 4, 256], F32, "od4"); b_od = Buf()
        sq4 = X.sb([128, 4, 256], F32, "sq4"); b_sq = Buf()
        odb4 = X.sb([128, 4, 256], BF16, "odb4"); b_odb = Buf()
        rs = [X.sb([128, 4, 1], F32, "rs") for _ in range(2)]; b_rs = [Buf(), Buf()]
        st4 = X.sb([128, 4, 4], F32, "st4"); b_st = Buf()
        stg = [X.sb([128, 2, 512], BF16, "stg") for _ in range(2)]; b_stg = [Buf(), Buf()]
        sps = [X.ps([128, 512], F32, "sps") for _ in range(3)]; b_sps = [Buf() for _ in range(3)]
        ops = X.ps([128, 4, 512], F32, "ops"); b_ops = [Buf() for _ in range(4)]
        tps = X.ps([128, 8, 128], BF16, "tps"); b_tps = Buf()
        for i in range(2):
            P.op("dve", lambda e, i=i: e.memset(vaug[i][:, :, 256:257], 1.0), writes=[b_v[i]])
        n_p = 0; n_q = 0; n_blk = 0
        pend = {1: [], 6: [], 12: []}

        def flush(kc):
            for f in pend[kc]:
                f()
            pend[kc] = []

        def epilogue(m, ob, b_ob, rsx, b_rsx, h, qb, si):
            bc = lambda ap: ap.to_broadcast([128, 4, 256])
            def stage_a():
                P.op("dve", lambda e: e.reciprocal(rsx[:], ob[:, :, 256:257]), reads=[b_ob], writes=[b_rsx])
                if m == 0:
                    P.op("dve", lambda e: e.tensor_tensor(o1[:], ob[:, :, 0:256], bc(rsx[:]), ALU.mult), reads=[b_ob, b_rsx], writes=[b_o1])
                    return
                P.op("dve", lambda e: e.tensor_scalar_mul(st4[:, :, 0:1], rsx[:], nlam), reads=[b_rsx, b_lv], writes=[b_st])
                P.op("dve", lambda e: e.tensor_tensor(od4[:], ob[:, :, 0:256], bc(st4[:, :, 0:1]), ALU.mult), reads=[b_ob, b_st], writes=[b_od])
                P.op("dve", lambda e: e.tensor_tensor(od4[:], od4[:], o1[:], ALU.add), reads=[b_od, b_o1], writes=[b_od])
                P.op("dve", lambda e: e.tensor_tensor(sq4[:], od4[:], od4[:], ALU.mult), reads=[b_od], writes=[b_sq])
                P.op("dve", lambda e: e.reduce_sum(st4[:, :, 1], sq4[:], AX.X), reads=[b_sq], writes=[b_st])
                P.op("dve", lambda e: e.tensor_scalar(st4[:, :, 1], st4[:, :, 1], 1.0 / 256, EPS, ALU.mult, ALU.add), reads=[b_st], writes=[b_st])
            def stage_b():
                if m == 0:
                    return
                P.op("act", lambda e: e.activation(st4[:, :, 2], st4[:, :, 1], AF.Ln), reads=[b_st], writes=[b_st])
                P.op("act", lambda e: e.activation(st4[:, :, 3], st4[:, :, 2], AF.Exp, scale=-0.5), reads=[b_st], writes=[b_st])
                P.op("dve", lambda e: e.tensor_tensor(od4[:], od4[:], bc(st4[:, :, 3:4]), ALU.mult), reads=[b_od, b_st], writes=[b_od])
                P.op("dve", lambda e: e.tensor_tensor(odb4[:], od4[:], sg[:].unsqueeze(1).to_broadcast([128, 4, 256]), ALU.mult),
                     reads=[b_od, b_sg], writes=[b_odb])
            def stage_c():
                if m == 0:
                    return
                for hf in range(2):
                    for s in range(4):
                        P.op("pe", lambda e, hf=hf, s=s: e.transpose(tps[:, hf * 4 + s, :], odb4[:, s, hf * 128:(hf + 1) * 128], ident_bf[:]),
                             reads=[b_odb, b_identbf], writes=[b_tps], inc=(hf == 1 and s == 3))
                P.op("act", lambda e: e.copy(stg[si][:], tps[:].rearrange("p (a b) c -> p a (b c)", a=2)), reads=[b_tps], writes=[b_stg[si]])
                P.dma("sp", lambda e: e.dma_start(out=odT[h * 2:h * 2 + 2, :, qb * 512:(qb + 1) * 512].rearrange("c p n -> p c n"),
                                                 in_=stg[si][:]), reads=[b_stg[si]])
            pend[1].append(stage_a); pend[6].append(stage_b); pend[12].append(stage_c)

        for h in range(4):
            vi = h % 2
            P.dma("sp", lambda e, h=h, vi=vi: e.dma_start(out=vaug[vi][:, :, 0:256],
                                                           in_=vd[:, h * 256:(h + 1) * 256].rearrange("(kc p) e -> p kc e", p=128)),
                  writes=[b_v[vi]])
            for m in range(2):
                ki = (h * 2 + m) % 4
                P.dma("sp", lambda e, h=h, m=m, ki=ki: e.dma_start(out=kt[ki][:], in_=kT[h * 2 + m]), writes=[b_kt[ki]])
            for qb in range(8):
                si = (h * 8 + qb) % 2
                for m in range(2):
                    ki = (h * 2 + m) % 4
                    qi = n_q % 4; n_q += 1
                    P.dma("sp", lambda e, h=h, m=m, qb=qb, qi=qi: e.dma_start(out=qb_t[qi][:], in_=qT[h * 2 + m, :, qb * 512:(qb + 1) * 512]),
                          writes=[b_qb[qi]])

                    def emit_qk(kc, ki=ki, qi=qi):
                        s_i = kc % 3
                        P.op("pe", lambda e, s_i=s_i, ki=ki, kc=kc, qi=qi: e.matmul(sps[s_i][:], kt[ki][:, kc * 128:(kc + 1) * 128], qb_t[qi][:],
                                                                                  start=True, stop=True),
                             reads=[b_kt[ki], b_qb[qi]], writes=[b_sps[s_i]])
                    emit_qk(0); emit_qk(1)
                    for kc in range(NT):
                        if kc + 2 < NT:
                            emit_qk(kc + 2)
                        s_i = kc % 3
                        p_i = n_p % 3; n_p += 1
                        P.op("act", lambda e, s_i=s_i, p_i=p_i: e.activation(pT[p_i][:], sps[s_i][:], AF.Exp, scale=scale),
                             reads=[b_sps[s_i]], writes=[b_pT[p_i]])
                        for s in range(4):
                            P.op("pe", lambda e, s=s, p_i=p_i, kc=kc, vi=vi: e.matmul(ops[:, s, 0:257], pT[p_i][:, s * 128:(s + 1) * 128], vaug[vi][:, kc, :],
                                                                                   start=(kc == 0), stop=(kc == NT - 1)),
                                 reads=[b_pT[p_i], b_v[vi]], writes=[b_ops[s]], inc=(s == 3))
                        if kc in pend:
                            flush(kc)
                    oi = n_blk % 2; n_blk += 1
                    for s in range(4):
                        P.op("act", lambda e, s=s, oi=oi: e.copy(osb[oi][:, s, :], ops[:, s, 0:257]), reads=[b_ops[s]], writes=[b_osb[oi]])
                    epilogue(m, osb[oi], b_osb[oi], rs[oi], b_rs[oi], h, qb, si)
        for kc in (1, 6, 12):
            flush(kc)
        P.barrier(); P.emit(); X.close()

    def phase_gla(l):
        [t._get() for t in (w2f, bfw, w2b, bbw, gla_g)]
        X = Ctx(nc)
        ball1 = X.sb([128, NT, 512], F32, "ball"); blast = X.sb([128, NT, 4], F32, "blast"); b_blast = [Buf() for _ in range(NT)]
        ball = [ball1, ball1, blast, b_blast]; b_ball = [[Buf() for _ in range(NT)] for _ in range(2)]
        for d in range(2):
            gla_decay(l, d, ball, b_ball)
            gla_scan(l, d, ball, b_ball)
        X.close()

    def gla_decay(l, d, ball, b_ball):
        Y = Ctx(nc)
        blast, b_blast = ball[2], ball[3]
        glt32 = Y.sb([16, S], F32, "glt32"); glt = Y.sb([16, S], BF16, "glt"); b_glt = Buf()
        w232 = Y.sb([16, 512], F32, "w232"); w2 = Y.sb([16, 512], BF16, "w2"); b_w2 = Buf()
        bias = Y.sb([128, 512], F32, "bias"); b_bias = Buf()
        tri = Y.sb([128, 128], BF16, "tri"); b_tri = Buf()
        wsrc, bsrc = (w2f, bfw) if d == 0 else (w2b, bbw)
        P.dma("sp", lambda e: e.dma_start(out=glt32[:], in_=glT[d * 16:(d + 1) * 16, :]), writes=[b_glt])
        P.op("dve", lambda e: e.tensor_copy(glt[:], glt32[:]), reads=[b_glt], writes=[b_glt])
        P.dma("sp", lambda e: e.dma_start(out=w232[:], in_=wsrc[l]), writes=[b_w2])
        P.op("dve", lambda e: e.tensor_copy(w2[:], w232[:]), reads=[b_w2], writes=[b_w2])
        bcast_load("sp", bias[:], b_bias, bsrc[l])
        tsrc = consts[:, C_TRIF:C_TRIF + 128] if d == 0 else consts[:, C_TRIB:C_TRIB + 128]
        P.op("dve", lambda e: e.tensor_copy(tri[:], tsrc), reads=[b_consts], writes=[b_tri])
        negblk = Y.sb([128, 8], BF16, "negblk")
        P.op("dve", lambda e: e.memset(negblk[:], -1.0 / 16.0), writes=[b_tri])
        zps = [Y.ps([128, 512], F32, "zps") for _ in range(2)]; b_zps = [Buf(), Buf()]
        bps = [Y.ps([128, 512], F32, "bps") for _ in range(2)]; b_bps = [Buf(), Buf()]
        lps = [Y.ps([128, 4, 8], F32, "lps") for _ in range(2)]; b_lps = [Buf(), Buf()]
        zs = [Y.sb([128, 512], F32, "zs") for _ in range(2)]; b_zs = [Buf(), Buf()]
        hi = [Y.sb([128, 512], BF16, "hi") for _ in range(2)]; lo = [Y.sb([128, 512], BF16, "lo") for _ in range(2)]
        b_hl = [Buf(), Buf()]
        def stage1(ci):
            i = ci % 2
            P.op("pe", lambda e: e.matmul(zps[i][:], glt[:, ci * 128:(ci + 1) * 128], w2[:], start=True, stop=True),
                 reads=[b_glt, b_w2], writes=[b_zps[i]])
            P.op("dve", lambda e: e.tensor_add(zs[i][:], zps[i][:], bias[:]), reads=[b_zps[i], b_bias], writes=[b_zs[i]])
            P.op("act", lambda e: e.activation(zs[i][:], zs[i][:], AF.Exp, scale=-1.0), reads=[b_zs[i]], writes=[b_zs[i]])
            P.op("act", lambda e: e.activation(zs[i][:], zs[i][:], AF.Ln, bias=1.0), reads=[b_zs[i]], writes=[b_zs[i]])
            P.op("dve", lambda e: e.tensor_copy(hi[i][:], zs[i][:]), reads=[b_zs[i]], writes=[b_hl[i]])
            P.op("dve", lambda e: e.tensor_tensor(lo[i][:], zs[i][:], hi[i][:], ALU.subtract), reads=[b_zs[i], b_hl[i]], writes=[b_hl[i]])

        def stage2(ci):
            i = ci % 2
            P.op("pe", lambda e: e.matmul(bps[i][:], tri[:], hi[i][:], start=True, stop=False), reads=[b_tri, b_hl[i]], writes=[b_bps[i]], inc=False)
            P.op("pe", lambda e: e.matmul(bps[i][:], tri[:], lo[i][:], start=False, stop=True), reads=[b_tri, b_hl[i]], writes=[b_bps[i]])
            for hh in range(4):
                P.op("pe", lambda e, hh=hh: e.matmul(lps[i][:, hh, :], hi[i][:, hh * 128:(hh + 1) * 128], negblk[:], start=True, stop=False),
                     reads=[b_tri, b_hl[i]], writes=[b_lps[i]], inc=False)
                P.op("pe", lambda e, hh=hh: e.matmul(lps[i][:, hh, :], lo[i][:, hh * 128:(hh + 1) * 128], negblk[:], start=False, stop=True),
                     reads=[b_tri, b_hl[i]], writes=[b_lps[i]], inc=(hh == 3))
            P.op("dve", lambda e: e.tensor_copy(ball[d][:, ci, :], bps[i][:]), reads=[b_bps[i]], writes=[b_ball[d][ci]])
            P.op("act", lambda e: e.copy(blast[:, ci, :], lps[i][:, :, 0]), reads=[b_lps[i]], writes=[b_blast[ci]])

        stage1(0)
        for ci in range(NT):
            if ci + 1 < NT:
                stage1(ci + 1)
            stage2(ci)
        P.barrier(); P.emit(); Y.close()

    def gla_scan(l, d, ball, b_ball):
        Y = Ctx(nc)
        NB = 2
        gq_t = [Y.sb([128, 512], F32, "gq") for _ in range(NB)]; gk_t = [Y.sb([128, 512], F32, "gk") for _ in range(NB)]
        gv_t = [Y.sb([128, 1024], BF16, "gv") for _ in range(NB)]; b_in = [Buf() for _ in range(NB)]
        eb = [Y.sb([128, 512], F32, "eb") for _ in range(NB)]; enb = [Y.sb([128, 512], F32, "enb") for _ in range(NB)]
        b_eb = [Buf() for _ in range(NB)]; b_enb = [Buf() for _ in range(NB)]
        qk = [Y.sb([128, 2, 512], BF16, "qk") for _ in range(NB)]; b_qk = [Buf() for _ in range(NB)]
        qkT = [Y.sb([128, 8, 128], BF16, "qkT") for _ in range(NB)]; b_qkT = [Buf() for _ in range(NB)]
        sT = [Y.sb([128, 4, 128], BF16, "sT") for _ in range(NB)]; b_sT = [Buf() for _ in range(NB)]
        ebl = [Y.sb([128, 4], F32, "ebl") for _ in range(NB)]; b_ebl = [Buf() for _ in range(NB)]
        St = Y.sb([128, 4, 256], F32, "St"); Sbf = Y.sb([128, 4, 256], BF16, "Sbf"); b_S = Buf(); b_Sbf = Buf()
        ostg = [Y.sb([128, 1024], F32, "ostg") for _ in range(NB)]; b_ostg = [Buf() for _ in range(NB)]
        oprev = [Y.sb([128, 1024], F32, "oprev") for _ in range(NB)]; b_oprev = [Buf() for _ in range(NB)]
        grt = [Y.sb([128, 1024], BF16, "grt") for _ in range(3)]; b_grt = [Buf() for _ in range(3)]
        sqt = Y.sb([128, 1024], F32, "sqt"); b_sqt = Buf()
        ogb = [Y.sb([128, 1024], BF16, "ogb") for _ in range(NB)]; b_ogb = [Buf() for _ in range(NB)]
        ogs = [Y.sb([128, 8, 128], BF16, "ogs") for _ in range(NB)]; b_ogs = [Buf() for _ in range(NB)]
        gg = Y.sb([128, 4, 256], F32, "gg"); b_gg = Buf()
        junk = Y.sb([128, 256], BF16, "junk")
        ssn = [Y.sb([128, 16], F32, "ssn") for _ in range(NB)]; b_ssn = [Buf() for _ in range(NB)]
        for hh in range(4):
            bcast_load("sp", gg[:, hh, :], b_gg, gla_g[l])
        tps = Y.ps([128, 8, 128], BF16, "tps"); b_tps = Buf()
        sps = Y.ps([128, 4, 128], F32, "sps"); b_sps = Buf()
        ops = Y.ps([128, 4, 256], F32, "ops"); b_ops = Buf()
        ups = Y.ps([128, 4, 256], F32, "ups"); b_ups = Buf()
        gps = Y.ps([128, 8, 128], BF16, "gps"); b_gps = Buf()
        ssb = Y.sb([128, 4, 128], F32, "ssb"); b_ssb = Buf()
        Sb2 = [Sbf, Y.sb([128, 4, 256], BF16, "Sbf2")]; b_Sb2 = [b_Sbf, Buf()]
        P.op("dve", lambda e: e.memset(St[:], 0.0), writes=[b_S])
        P.op("dve", lambda e: e.memset(Sb2[0][:], 0.0), writes=[b_Sb2[0]])
        mask = consts[:, C_MF:C_MF + 128] if d == 0 else consts[:, C_MB:C_MB + 128]

        def stage_a(cc):
            ci = cc if d == 0 else NT - 1 - cc
            i = cc % NB
            tok = slice(ci * 128, (ci + 1) * 128)
            P.dma("sp", lambda e: e.dma_start(out=gq_t[i][:], in_=gq[tok, :]), writes=[b_in[i]])
            P.dma("sp", lambda e: e.dma_start(out=gk_t[i][:], in_=gk[tok, :]), writes=[b_in[i]])
            P.dma("sp", lambda e: e.dma_start(out=gv_t[i][:], in_=gv[tok, :]), writes=[b_in[i]])
            if d == 1:
                P.dma("sp", lambda e: e.dma_start(out=oprev[i][:], in_=of_[tok, :]), writes=[b_oprev[i]])
                P.dma("sp", lambda e: e.dma_start(out=grt[cc % 3][:], in_=gr[tok, :]), writes=[b_grt[cc % 3]])
            bsrc = ball[d][:, ci, :]
            P.op("act", lambda e: e.activation(eb[i][:], bsrc, AF.Exp), reads=[b_ball[d][ci]], writes=[b_eb[i]])
            P.op("act", lambda e: e.activation(enb[i][:], bsrc, AF.Exp, scale=-1.0), reads=[b_ball[d][ci]], writes=[b_enb[i]])
            P.op("act", lambda e: e.activation(ebl[i][:], ball[2][:, ci, :], AF.Exp), reads=[ball[3][ci]], writes=[b_ebl[i]])
            P.op("dve", lambda e: e.scalar_tensor_tensor(qk[i][:, 0, :], gq_t[i][:], 128 ** -0.5, eb[i][:], ALU.mult, ALU.mult),
                 reads=[b_in[i], b_eb[i]], writes=[b_qk[i]])
            P.op("dve", lambda e: e.tensor_tensor(qk[i][:, 1, :], gk_t[i][:], enb[i][:], ALU.mult),
                 reads=[b_in[i], b_enb[i]], writes=[b_qk[i]])
            for j in range(8):
                P.op("pe", lambda e, j=j: e.transpose(tps[:, j, :], qk[i][:, j // 4, (j % 4) * 128:(j % 4 + 1) * 128], ident_bf[:]),
                     reads=[b_qk[i], b_identbf], writes=[b_tps], inc=(j == 7))
            P.op("act", lambda e: e.copy(qkT[i][:], tps[:]), reads=[b_tps], writes=[b_qkT[i]])
            for hh in range(4):
                P.op("pe", lambda e, hh=hh: e.matmul(sps[:, hh, :], qkT[i][:, 4 + hh, :], qkT[i][:, hh, :], start=True, stop=True),
                     reads=[b_qkT[i]], writes=[b_sps], inc=(hh == 3))
            P.op("act", lambda e: e.copy(ssb[:], sps[:]), reads=[b_sps], writes=[b_ssb])
            P.op("dve", lambda e: e.tensor_tensor(sT[i][:], ssb[:], mask.unsqueeze(1).to_broadcast([128, 4, 128]), ALU.mult),
                 reads=[b_ssb, b_consts], writes=[b_sT[i]])

        def stage_b(cc):
            ci = cc if d == 0 else NT - 1 - cc
            i = cc % NB
            tok = slice(ci * 128, (ci + 1) * 128)
            so, sn_ = cc % 2, (cc + 1) % 2
            for hh in range(4):
                P.op("pe", lambda e, hh=hh: e.matmul(ups[:, hh, :], qk[i][:, 1, hh * 128:(hh + 1) * 128], gv_t[i][:, hh * 256:(hh + 1) * 256],
                                                    start=True, stop=True),
                     reads=[b_qk[i], b_in[i]], writes=[b_ups], inc=(hh == 3))
            for hp in range(2):
                P.op("dve", lambda e, hp=hp: e.tensor_tensor(St[:, 2 * hp:2 * hp + 2, :], St[:, 2 * hp:2 * hp + 2, :], ups[:, 2 * hp:2 * hp + 2, :], ALU.add),
                     reads=[b_ups, b_S], writes=[b_S])
            P.op("dve", lambda e: e.tensor_tensor(St[:], St[:], ebl[i][:].unsqueeze(2).to_broadcast([128, 4, 256]), ALU.mult),
                 reads=[b_ebl[i], b_S], writes=[b_S])
            P.op("act", lambda e: e.copy(Sb2[sn_][:], St[:]), reads=[b_S], writes=[b_Sb2[sn_]])
            for hh in range(4):
                P.op("pe", lambda e, hh=hh: e.matmul(ops[:, hh, :], sT[i][:, hh, :], gv_t[i][:, hh * 256:(hh + 1) * 256], start=True, stop=False),
                     reads=[b_sT[i], b_in[i]], writes=[b_ops], inc=False)
                P.op("pe", lambda e, hh=hh: e.matmul(ops[:, hh, :], qkT[i][:, hh, :], Sb2[so][:, hh, :], start=False, stop=True),
                     reads=[b_qkT[i], b_Sb2[so]], writes=[b_ops], inc=(hh == 3))
            if d == 0:
                for hp in range(2):
                    P.op("act", lambda e, hp=hp: e.copy(ostg[i][:, hp * 512:(hp + 1) * 512], ops[:, 2 * hp:2 * hp + 2, :].rearrange("p h e -> p (h e)")),
                         reads=[b_ops], writes=[b_ostg[i]])
                P.dma("pool", lambda e: e.dma_start(out=of_[tok, :], in_=ostg[i][:]), reads=[b_ostg[i]])
            else:
                for hp in range(2):
                    P.op("dve", lambda e, hp=hp: e.tensor_tensor(ostg[i][:, hp * 512:(hp + 1) * 512], ops[:, 2 * hp:2 * hp + 2, :].rearrange("p h e -> p (h e)"),
                                                                oprev[i][:, hp * 512:(hp + 1) * 512], ALU.add),
                         reads=[b_ops, b_oprev[i]], writes=[b_ostg[i]])

        def stage_c(cc):
            if d == 0:
                return
            ci = NT - 1 - cc
            i = cc % NB
            tok = slice(ci * 128, (ci + 1) * 128)
            o3 = ostg[i][:].rearrange("p (h e) -> p h e", h=4)
            sq3 = sqt[:].rearrange("p (h e) -> p h e", h=4)
            P.op("dve", lambda e: e.tensor_tensor(sqt[:], ostg[i][:], ostg[i][:], ALU.mult), reads=[b_ostg[i]], writes=[b_sqt])
            P.op("dve", lambda e: e.reduce_sum(ssn[i][:, 0:4], sq3, AX.X), reads=[b_sqt], writes=[b_ssn[i]])
            P.op("dve", lambda e: e.tensor_scalar(ssn[i][:, 4:8], ssn[i][:, 0:4], 1.0 / 256, EPS, ALU.mult, ALU.add), reads=[b_ssn[i]], writes=[b_ssn[i]])
            P.op("act", lambda e: e.activation(ssn[i][:, 8:12], ssn[i][:, 4:8], AF.Ln), reads=[b_ssn[i]], writes=[b_ssn[i]])
            P.op("act", lambda e: e.activation(ssn[i][:, 12:16], ssn[i][:, 8:12], AF.Exp, scale=-0.5), reads=[b_ssn[i]], writes=[b_ssn[i]])
            P.op("dve", lambda e: e.tensor_tensor(o3, o3, ssn[i][:, 12:16].unsqueeze(2).to_broadcast([128, 4, 256]), ALU.mult),
                 reads=[b_ssn[i], b_ostg[i]], writes=[b_ostg[i]])
            P.op("dve", lambda e: e.tensor_tensor(o3, o3, gg[:], ALU.mult), reads=[b_gg, b_ostg[i]], writes=[b_ostg[i]])
            P.op("dve", lambda e: e.tensor_tensor(ogb[i][:], ostg[i][:], grt[cc % 3][:], ALU.mult), reads=[b_ostg[i], b_grt[cc % 3]], writes=[b_ogb[i]])
            for j in range(8):
                P.op("pe", lambda e, j=j: e.transpose(gps[:, j, :], ogb[i][:, j * 128:(j + 1) * 128], ident_bf[:]),
                     reads=[b_ogb[i], b_identbf], writes=[b_gps], inc=(j == 7))
            P.op("act", lambda e: e.copy(ogs[i][:], gps[:]), reads=[b_gps], writes=[b_ogs[i]])
            P.dma("pool", lambda e: e.dma_start(out=ogT[:, :, tok].rearrange("c p n -> p c n"), in_=ogs[i][:]), reads=[b_ogs[i]])

        stage_a(0)
        for cc in range(NT):
            if cc + 1 < NT:
                stage_a(cc + 1)
            stage_b(cc)
            if cc >= 1:
                stage_c(cc - 1)
        stage_c(NT - 1)
        P.barrier(); P.emit(); Y.close()

    def phase_merge(l):
        [t._get() for t in (w_bd, w_bg)]
        X = Ctx(nc)
        wbd = X.sb([128, 8, D], BF16, "wbd"); wbg = X.sb([128, 8, D], BF16, "wbg")
        b_wd8 = [Buf() for _ in range(8)]; b_wg8 = [Buf() for _ in range(8)]
        for ec in range(8):
            P.dma("pool", lambda e, ec=ec: e.dma_start(out=wbd[:, ec, :], in_=w_bd[l][ec * 128:(ec + 1) * 128, :]), writes=[b_wd8[ec]])
            P.dma("pool", lambda e, ec=ec: e.dma_start(out=wbg[:, ec, :], in_=w_bg[l][ec * 128:(ec + 1) * 128, :]), writes=[b_wg8[ec]])
        odt = [X.sb([128, 8, 512], BF16, "odt") for _ in range(2)]; ogt = [X.sb([128, 8, 512], BF16, "ogt") for _ in range(2)]
        b_o = [Buf(), Buf()]
        gat = X.sb([128, 16, 512], BF16, "gat"); gbt = X.sb([128, 16, 512], BF16, "gbt"); b_g = Buf()
        mt = [X.sb([128, 16, 512], BF16, "mt") for _ in range(2)]; b_mt = [Buf(), Buf()]
        sa = [X.sb([128, 512], F32, "sa") for _ in range(2)]; sb_ = [X.sb([128, 512], F32, "sb") for _ in range(2)]
        b_sa = [Buf(), Buf()]; b_sb = [Buf(), Buf()]
        yd = [X.ps([128, 512], F32, "yd") for _ in range(2)]; yg = [X.ps([128, 512], F32, "yg") for _ in range(2)]
        b_yd = [Buf(), Buf()]; b_yg = [Buf(), Buf()]
        n = 0
        for tb in range(8):
            i = tb % 2
            tok = slice(tb * 512, (tb + 1) * 512)
            P.dma("sp", lambda e, i=i, tok=tok: e.dma_start(out=odt[i][:], in_=odT[:, :, tok].rearrange("c p n -> p c n")), writes=[b_o[i]])
            P.dma("sp", lambda e, i=i, tok=tok: e.dma_start(out=ogt[i][:], in_=ogT[:, :, tok].rearrange("c p n -> p c n")), writes=[b_o[i]])
            P.dma("sp", lambda e, tok=tok: e.dma_start(out=gat[:], in_=gabT[0:16, :, tok].rearrange("c p n -> p c n")), writes=[b_g])
            P.dma("sp", lambda e, tok=tok: e.dma_start(out=gbt[:], in_=gabT[16:32, :, tok].rearrange("c p n -> p c n")), writes=[b_g])
            for dc in range(16):
                j = n % 2; n += 1
                for ec in range(8):
                    P.op("pe", lambda e, j=j, ec=ec, dc=dc, i=i: e.matmul(yd[j][:], wbd[:, ec, dc * 128:(dc + 1) * 128], odt[i][:, ec, :],
                                                                        start=(ec == 0), stop=(ec == 7)),
                         reads=[b_wd8[ec], b_o[i]], writes=[b_yd[j]], inc=(ec == 7))
                for ec in range(8):
                    P.op("pe", lambda e, j=j, ec=ec, dc=dc, i=i: e.matmul(yg[j][:], wbg[:, ec, dc * 128:(dc + 1) * 128], ogt[i][:, ec, :],
                                                                        start=(ec == 0), stop=(ec == 7)),
                         reads=[b_wg8[ec], b_o[i]], writes=[b_yg[j]], inc=(ec == 7))
                P.op("act", lambda e, j=j, dc=dc: e.activation(sa[j][:], gat[:, dc, :], AF.Sigmoid), reads=[b_g], writes=[b_sa[j]])
                P.op("act", lambda e, j=j, dc=dc: e.activation(sb_[j][:], gbt[:, dc, :], AF.Sigmoid), reads=[b_g], writes=[b_sb[j]])
                P.op("dve", lambda e, j=j: e.tensor_tensor(sa[j][:], sa[j][:], yd[j][:], ALU.mult), reads=[b_yd[j], b_sa[j]], writes=[b_sa[j]])
                P.op("dve", lambda e, j=j: e.tensor_tensor(sb_[j][:], sb_[j][:], yg[j][:], ALU.mult), reads=[b_yg[j], b_sb[j]], writes=[b_sb[j]])
                P.op("dve", lambda e, j=j, i=i, dc=dc: e.tensor_tensor(mt[i][:, dc, :], sa[j][:], sb_[j][:], ALU.add),
                     reads=[b_sa[j], b_sb[j]], writes=[b_mt[i]])
            P.dma("pool", lambda e, i=i, tok=tok: e.dma_start(out=mT[:, :, tok].rearrange("c p n -> p c n"), in_=mt[i][:]), reads=[b_mt[i]])
        P.barrier(); P.emit(); X.close()

    def phase_outproj(l, xsrc):
        [t._get() for t in (w_out, x_in)]
        X = Ctx(nc)
        wo = X.sb([128, KC, D], BF16, "wo"); b_wo = [Buf() for _ in range(KC)]
        for kc in range(KC):
            P.dma("pool", lambda e, kc=kc: e.dma_start(out=wo[:, kc, :], in_=w_out[l][kc * 128:(kc + 1) * 128, :]), writes=[b_wo[kc]])
        G1b = X.sb([128, D], F32, "G1b"); A2b = X.sb([128, D], F32, "A2b"); B2b = X.sb([128, D], F32, "B2b"); b_ab = Buf()
        bcast_load("sp", G1b[:], b_ab, modrows[l, 2]); bcast_load("sp", A2b[:], b_ab, modrows[l, 3]); bcast_load("sp", B2b[:], b_ab, modrows[l, 4])
        mt = [X.sb([128, KC, 512], BF16, "mt") for _ in range(2)]; b_mt = [Buf(), Buf()]
        xt = [X.sb([128, D], F32, "xt") for _ in range(3)]; b_xt = [Buf() for _ in range(3)]
        tmp = [X.sb([128, 512], F32, "tmp") for _ in range(2)]; b_tm = [Buf(), Buf()]
        hbf = [X.sb([128, D], BF16, "hbf") for _ in range(3)]; b_hbf = [Buf() for _ in range(3)]
        h2s = [X.sb([128, KC, 512], BF16, "h2s") for _ in range(2)]; b_h2s = [Buf(), Buf()]
        junk = X.sb([128, D], BF16, "junk"); ss = [X.sb([128, 4], F32, "ss") for _ in range(3)]; b_ss = [Buf() for _ in range(3)]
        yps = [X.ps([128, 512], F32, "yps") for _ in range(3)]; b_yps = [Buf() for _ in range(3)]
        tp = [X.ps([128, KC, 128], BF16, "tp") for _ in range(2)]; b_tp = [Buf(), Buf()]
        ny = 0; nt_ = 0
        pend = []
        for tb in range(8):
            i = tb % 2
            tok = slice(tb * 512, (tb + 1) * 512)
            P.dma("sp", lambda e, i=i, tok=tok: e.dma_start(out=mt[i][:], in_=mT[:, :, tok].rearrange("c p n -> p c n")), writes=[b_mt[i]])
            for s in range(4):
                xi = nt_ % 3; nt_ += 1
                t0 = tb * 512 + s * 128
                P.dma("sp", lambda e, xi=xi, t0=t0: e.dma_start(out=xt[xi][:], in_=xsrc[t0:t0 + 128, :]), writes=[b_xt[xi]])
                for cg in range(4):
                    yi = ny % 3; ti = ny % 2; ny += 1
                    for kc in range(KC):
                        P.op("pe", lambda e, yi=yi, kc=kc, i=i, s=s, cg=cg: e.matmul(yps[yi][:], mt[i][:, kc, s * 128:(s + 1) * 128],
                                                                                   wo[:, kc, cg * 512:(cg + 1) * 512], start=(kc == 0), stop=(kc == KC - 1)),
                             reads=[b_mt[i], b_wo[kc]], writes=[b_yps[yi]], inc=(kc == KC - 1))
                    cs = slice(cg * 512, (cg + 1) * 512)
                    P.op("dve", lambda e, yi=yi, ti=ti, cs=cs: e.tensor_tensor(tmp[ti][:], yps[yi][:], G1b[:, cs], ALU.mult),
                         reads=[b_yps[yi], b_ab], writes=[b_tm[ti]])
                    P.op("dve", lambda e, xi=xi, ti=ti, cs=cs: e.tensor_tensor(xt[xi][:, cs], xt[xi][:, cs], tmp[ti][:], ALU.add),
                         reads=[b_tm[ti], b_xt[xi]], writes=[b_xt[xi]])
                P.dma("pool", lambda e, xi=xi, t0=t0: e.dma_start(out=xa[t0:t0 + 128, :], in_=xt[xi][:]), reads=[b_xt[xi]])
                rms_mod(xt[xi], b_xt[xi], A2b, B2b, b_ab, hbf[xi], b_hbf[xi], ss[xi], junk, b_ss[xi])

                def later(xi=xi, i=i, s=s, tb=tb, tok=tok):
                    pi = (tb * 4 + s) % 2
                    for kc in range(KC):
                        P.op("pe", lambda e, kc=kc: e.transpose(tp[pi][:, kc, :], hbf[xi][:, kc * 128:(kc + 1) * 128], ident_bf[:]),
                             reads=[b_hbf[xi], b_identbf], writes=[b_tp[pi]], inc=(kc == KC - 1))
                    P.op("act", lambda e: e.copy(h2s[i][:, :, s * 128:(s + 1) * 128], tp[pi][:]), reads=[b_tp[pi]], writes=[b_h2s[i]])
                    if s == 3:
                        P.dma("pool", lambda e: e.dma_start(out=h2T[:, :, tok].rearrange("c p n -> p c n"), in_=h2s[i][:]), reads=[b_h2s[i]])
                pend.append(later)
                if len(pend) > 1:
                    pend.pop(0)()
        for f in pend:
            f()
        P.barrier(); P.emit(); X.close()

    def phase_ffn_up(l):
        [t._get() for t in (conv_w, conv_b, w_gate, w_up)]
        X = Ctx(nc)
        HW = 2049
        hs = X.sb([128, KC, HW], BF16, "hs"); b_hs = Buf()
        cw = X.sb([128, 4, NFC], F32, "cw"); b_cw = Buf()
        for k in range(4):
            vec = conv_w[l, k] if k < 3 else conv_b[l]
            for q4 in range(4):
                P.dma("sp", lambda e, k=k, q4=q4, vec=vec: e.dma_start(
                    out=cw[:, k, q4 * 11:(q4 + 1) * 11], in_=vec[q4 * 1408:(q4 + 1) * 1408].rearrange("(c p) -> p c", p=128),
                    allow_slow_non_contiguous=True), writes=[b_cw])
        wg = [X.sb([128, KC, 512], BF16, "wg") for _ in range(2)]; wu = [X.sb([128, KC, 512], BF16, "wu") for _ in range(2)]
        b_wg4 = [[Buf() for _ in range(4)] for _ in range(2)]; b_wu4 = [[Buf() for _ in range(4)] for _ in range(2)]
        gfull = [X.sb([128, HW + 2], F32, "gfull") for _ in range(2)]; b_gf = [Buf(), Buf()]
        usb = [X.sb([128, HW], BF16, "usb") for _ in range(2)]; b_u = [Buf(), Buf()]
        tmp = [X.sb([128, 2048], F32, "tmp") for _ in range(2)]; b_tmp = [Buf(), Buf()]
        asb = [X.sb([128, 2048], BF16, "asb") for _ in range(2)]; b_a = [Buf(), Buf()]
        gps = [X.ps([128, 512], F32, "gps") for _ in range(2)]; ups = [X.ps([128, 512], F32, "ups") for _ in range(2)]
        b_gps = [Buf(), Buf()]; b_ups = [Buf(), Buf()]
        for i in range(2):
            P.op("dve", lambda e, i=i: e.memset(gfull[i][:], 0.0), writes=[b_gf[i]])
        blocks = [(0, 410), (410, 410), (820, 410), (1230, 410), (1640, 409)]
        nw = 0; nf = 0; npb = 0
        for hf in range(2):
            t_lo = 0 if hf == 0 else 2047
            P.dma("sp", lambda e, t_lo=t_lo: e.dma_start(out=hs[:], in_=h2T[:, :, t_lo:t_lo + HW].rearrange("c p n -> p c n")), writes=[b_hs])
            o_li = 0 if hf == 0 else 1
            for fg in range(NFC // 4):
                wi = nw % 2; nw += 1
                csl = slice(fg * 512, (fg + 1) * 512)
                for q4 in range(4):
                    P.dma("pool", lambda e, wi=wi, csl=csl, q4=q4: e.dma_start(out=wg[wi][:, q4 * 4:(q4 + 1) * 4, :],
                                                                               in_=w_gate[l][q4 * 512:(q4 + 1) * 512, csl].rearrange("(kc p) n -> p kc n", p=128)), writes=[b_wg4[wi][q4]])
                for q4 in range(4):
                    P.dma("pool", lambda e, wi=wi, csl=csl, q4=q4: e.dma_start(out=wu[wi][:, q4 * 4:(q4 + 1) * 4, :],
                                                                               in_=w_up[l][q4 * 512:(q4 + 1) * 512, csl].rearrange("(kc p) n -> p kc n", p=128)), writes=[b_wu4[wi][q4]])
                for j in range(4):
                    fc = fg * 4 + j
                    fi = nf % 2; nf += 1
                    for (c0, n) in blocks:
                        pi = npb % 2; npb += 1
                        for kc in range(KC):
                            P.op("pe", lambda e, pi=pi, kc=kc, wi=wi, j=j, c0=c0, n=n: e.matmul(gps[pi][:, 0:n], wg[wi][:, kc, j * 128:(j + 1) * 128], hs[:, kc, c0:c0 + n],
                                                                                              start=(kc == 0), stop=(kc == KC - 1)),
                                 reads=[b_wg4[wi][kc // 4], b_hs], writes=[b_gps[pi]], inc=(kc == KC - 1))
                        for kc in range(KC):
                            P.op("pe", lambda e, pi=pi, kc=kc, wi=wi, j=j, c0=c0, n=n: e.matmul(ups[pi][:, 0:n], wu[wi][:, kc, j * 128:(j + 1) * 128], hs[:, kc, c0:c0 + n],
                                                                                              start=(kc == 0), stop=(kc == KC - 1)),
                                 reads=[b_wu4[wi][kc // 4], b_hs], writes=[b_ups[pi]], inc=(kc == KC - 1))
                        P.op("act", lambda e, pi=pi, fi=fi, c0=c0, n=n: e.copy(gfull[fi][:, 1 + c0:1 + c0 + n], gps[pi][:, 0:n]), reads=[b_gps[pi]], writes=[b_gf[fi]])
                        P.op("dve", lambda e, pi=pi, fi=fi, c0=c0, n=n: e.tensor_copy(usb[fi][:, c0:c0 + n], ups[pi][:, 0:n]), reads=[b_ups[pi]], writes=[b_u[fi]])
                    p0 = o_li + 1
                    P.op("dve", lambda e, fi=fi, fc=fc, p0=p0: e.tensor_scalar(tmp[fi][:], gfull[fi][:, p0:p0 + 2048], cw[:, 1, fc:fc + 1], cw[:, 3, fc:fc + 1], ALU.mult, ALU.add),
                         reads=[b_gf[fi], b_cw], writes=[b_tmp[fi]])
                    P.op("dve", lambda e, fi=fi, fc=fc, p0=p0: e.scalar_tensor_tensor(tmp[fi][:], gfull[fi][:, p0 - 1:p0 - 1 + 2048], cw[:, 0, fc:fc + 1], tmp[fi][:], ALU.mult, ALU.add),
                         reads=[b_gf[fi], b_cw, b_tmp[fi]], writes=[b_tmp[fi]])
                    P.op("dve", lambda e, fi=fi, fc=fc, p0=p0: e.scalar_tensor_tensor(tmp[fi][:], gfull[fi][:, p0 + 1:p0 + 1 + 2048], cw[:, 2, fc:fc + 1], tmp[fi][:], ALU.mult, ALU.add),
                         reads=[b_gf[fi], b_cw, b_tmp[fi]], writes=[b_tmp[fi]])
                    P.op("act", lambda e, fi=fi: e.activation(tmp[fi][:], tmp[fi][:], AF.Silu), reads=[b_tmp[fi]], writes=[b_tmp[fi]])
                    P.op("dve", lambda e, fi=fi, o_li=o_li: e.tensor_tensor(asb[fi][:], tmp[fi][:], usb[fi][:, o_li:o_li + 2048], ALU.mult),
                         reads=[b_tmp[fi], b_u[fi]], writes=[b_a[fi]])
                    P.dma("sp", lambda e, fi=fi, fc=fc, hf=hf: e.dma_start(out=aT[fc, :, hf * 2048:(hf + 1) * 2048], in_=asb[fi][:]), reads=[b_a[fi]])
        P.barrier(); P.emit(); X.close()

    def phase_ffn_down(l, xsrc, xdst):
        w_down._get()
        X = Ctx(nc)
        wd = X.sb([128, NFC, 1024], BF16, "wd"); b_wd = [Buf() for _ in range(4)]
        G2b = X.sb([128, D], F32, "G2b"); b_g2 = Buf()
        bcast_load("sp", G2b[:], b_g2, modrows[l, 5])
        at = [X.sb([128, NFC, 256], BF16, "at") for _ in range(3)]; b_at = [Buf() for _ in range(3)]
        xt = [X.sb([128, 512], F32, "xt") for _ in range(3)]; b_xt = [Buf() for _ in range(3)]
        tmp = [X.sb([128, 512], F32, "tmp") for _ in range(2)]; b_tm = [Buf(), Buf()]
        yps = [X.ps([128, 512], F32, "yps") for _ in range(3)]; b_yps = [Buf() for _ in range(3)]
        n = 0
        for cb in range(2):
            for q4 in range(4):
                fsl = slice(q4 * 11, (q4 + 1) * 11)
                P.dma("pool", lambda e, cb=cb, q4=q4, fsl=fsl: e.dma_start(
                    out=wd[:, fsl, :], in_=w_down[l][q4 * 11 * 128:(q4 + 1) * 11 * 128, cb * 1024:(cb + 1) * 1024].rearrange("(fc p) n -> p fc n", p=128)),
                    writes=[b_wd[q4]])
            for tb in range(16):
                i = (cb * 16 + tb) % 3
                tok = slice(tb * 256, (tb + 1) * 256)
                P.dma("pool", lambda e, i=i, tok=tok: e.dma_start(out=at[i][:], in_=aT[:, :, tok].rearrange("c p n -> p c n")), writes=[b_at[i]])
                for s in range(2):
                    t0 = tb * 256 + s * 128
                    for cg in range(2):
                        yi = n % 3; ti = n % 2; xi = n % 3; n += 1
                        c0 = cb * 1024 + cg * 512
                        P.dma("sp", lambda e, xi=xi, t0=t0, c0=c0: e.dma_start(out=xt[xi][:], in_=xsrc[t0:t0 + 128, c0:c0 + 512]), writes=[b_xt[xi]])
                        for fc in range(NFC):
                            P.op("pe", lambda e, yi=yi, fc=fc, i=i, s=s, cg=cg: e.matmul(yps[yi][:], at[i][:, fc, s * 128:(s + 1) * 128],
                                                                                       wd[:, fc, cg * 512:(cg + 1) * 512], start=(fc == 0), stop=(fc == NFC - 1)),
                                 reads=[b_at[i], b_wd[fc // 11]], writes=[b_yps[yi]], inc=(fc == NFC - 1))
                        P.op("dve", lambda e, yi=yi, ti=ti, c0=c0: e.tensor_tensor(tmp[ti][:], yps[yi][:], G2b[:, c0:c0 + 512], ALU.mult),
                             reads=[b_yps[yi], b_g2], writes=[b_tm[ti]])
                        P.op("dve", lambda e, xi=xi, ti=ti: e.tensor_tensor(xt[xi][:], xt[xi][:], tmp[ti][:], ALU.add),
                             reads=[b_tm[ti], b_xt[xi]], writes=[b_xt[xi]])
                        P.dma("sp", lambda e, xi=xi, t0=t0, c0=c0: e.dma_start(out=xdst[t0:t0 + 128, c0:c0 + 512], in_=xt[xi][:]), reads=[b_xt[xi]])
        P.barrier(); P.emit(); X.close()

    def phase_final(xsrc):
        fin_g._get()
        X = Ctx(nc)
        gb = X.sb([128, D], F32, "gb"); b_gb = Buf()
        bcast_load("sp", gb[:], b_gb, fin_g[:])
        xt = [X.sb([128, D], F32, "xt") for _ in range(3)]; b_xt = [Buf() for _ in range(3)]
        junk = X.sb([128, D], BF16, "junk"); ss = [X.sb([128, 4], F32, "ss") for _ in range(3)]; b_ss = [Buf() for _ in range(3)]
        for t in range(NT):
            i = t % 3
            P.dma("sp", lambda e, i=i, t=t: e.dma_start(out=xt[i][:], in_=xsrc[t * 128:(t + 1) * 128, :]), writes=[b_xt[i]])
            rms_mod(xt[i], b_xt[i], gb, None, b_gb, None, None, ss[i], junk, b_ss[i])
            P.dma("pool", lambda e, i=i, t=t: e.dma_start(out=out[t * 128:(t + 1) * 128, :], in_=xt[i][:]), reads=[b_xt[i]])
        P.barrier(); P.emit(); X.close()

    P.barrier(); P.emit()
    phases = []
    phases.append(("rotary", phase_rotary))
    phases.append(("adaln", phase_adaln))
    for l in range(nlayers):
        xsrc = x_in if l == 0 else xb
        phases.append((f"inproj{l}", lambda l=l, xsrc=xsrc: phase_inproj(l, xsrc)))
        phases.append((f"diff{l}", lambda l=l: phase_diffattn(l)))
        phases.append((f"gla{l}", lambda l=l: phase_gla(l)))
        phases.append((f"merge{l}", lambda l=l: phase_merge(l)))
        phases.append((f"outproj{l}", lambda l=l, xsrc=xsrc: phase_outproj(l, xsrc)))
        phases.append((f"ffnup{l}", lambda l=l: phase_ffn_up(l)))
        phases.append((f"ffndown{l}", lambda l=l: phase_ffn_down(l, xa, xb)))
    phases.append(("final", lambda: phase_final(xb)))
    for name, fn in phases:
        if name in skip or (only is not None and name not in only):
            continue
        fn()
        if stop is not None and (name == stop or name + "a" == stop or name + "b" == stop):
            break
    G.close()
    nc._used_inputs = used_inputs + used_scratch_in
    return nc, P


def make_consts():
    c = np.zeros((128, NCONST), np.float32)
    j = np.arange(128)[:, None]
    i = np.arange(128)[None, :]
    c[:, C_ID:C_ID + 128] = np.eye(128, dtype=np.float32)
    c[:, C_TRIF:C_TRIF + 128] = np.where(j <= i, -1.0 / 16.0, 0.0)
    c[:, C_TRIB:C_TRIB + 128] = np.where(j >= i, -1.0 / 16.0, 0.0)
    c[:, C_MF:C_MF + 128] = (j <= i).astype(np.float32)
    c[:, C_MB:C_MB + 128] = (j >= i).astype(np.float32)
    c[127, C_OL] = 1.0
    c[0, C_OF] = 1.0
    inv_freq = (500000.0 ** (-np.arange(0, 32, 2, dtype=np.float32) / 32.0)).astype(np.float32)
    c[:, C_INVF:C_INVF + 16] = inv_freq[None, :]
    return c


_WNAMES = ["w_ada", "b_ada", "norm_mix_g", "w_in", "lambda_q1", "lambda_k1", "lambda_q2", "lambda_k2",
           "diff_subln_g", "gla_w2_fwd", "gla_b_fwd", "gla_w2_bwd", "gla_b_bwd", "gla_norm_g",
           "w_branch_diff", "w_branch_gla", "w_out", "norm_ffn_g", "w_gate", "w_up", "conv_w", "conv_b",
           "w_down", "final_norm_g"]


def make_in_maps(inputs):
    consts = make_consts()
    shared = {n: np.ascontiguousarray(np.asarray(inputs[n], dtype=np.float32)) for n in _WNAMES}
    x = np.asarray(inputs["x"], dtype=np.float32)
    c = np.asarray(inputs["c"], dtype=np.float32)
    pos = np.asarray(inputs["positions"]).astype(np.int32)
    in_maps = []
    for b in range(8):
        m = dict(shared)
        m["x"] = np.ascontiguousarray(x[b])
        m["c"] = np.ascontiguousarray(c[b].reshape(128, 16))
        m["pos"] = np.ascontiguousarray(pos[b].reshape(NT, 128).T)
        m["consts"] = consts
        in_maps.append(m)
    return in_maps


def kernel(**inputs):
    nc, _ = build()
    in_maps = [{k: v for k, v in m.items() if k in nc._used_inputs} for m in make_in_maps(inputs)]
    res = run_bass_kernel_spmd(nc, in_maps, core_ids=list(range(8)))
    return np.stack([np.asarray(r["out"], dtype=np.float32) for r in res.results], axis=0)
```

```python
import contextlib
import math
import numpy as np
import concourse.bass as bass
import concourse.mybir as mybir
from concourse.bass_utils import run_bass_kernel_spmd

F32 = mybir.dt.float32
BF16 = mybir.dt.bfloat16
I32 = mybir.dt.int32
AF = mybir.ActivationFunctionType
ALU = mybir.AluOpType
AX = mybir.AxisListType

D = 2048
S = 4096
L = 2
FF = 5632
NT = S // 128
KC = D // 128
NFC = FF // 128
IN_COLS = 10272
EPS = 1e-6
SEM_LIMIT = 30000
SYNC_SAME = True
TWO_PI = 2.0 * math.pi

C_ID, C_TRIF, C_TRIB, C_MF, C_MB, C_OL, C_OF, C_INVF = 0, 128, 256, 384, 512, 640, 641, 642
NCONST = 658


class Buf:
    __slots__ = ("w", "r")

    def __init__(self):
        self.w = None
        self.r = []


class Prog:
    COMPUTE = ("pe", "act", "dve", "pool")

    def __init__(self, nc, sync_same_engine=SYNC_SAME, n_dma_sems=16):
        self.nc = nc
        self.sync_same = sync_same_engine
        self.queues = {"pe": [], "act": [], "dve": [], "pool": [], "sp": []}
        self._ctx = []
        self.sems = {}
        self.cur = {}
        self.waited = {q: {} for q in self.queues}
        for e in self.COMPUTE:
            self._new_epoch(e)
        self.dma_ring = {}
        for q in ("sp", "pool", "act"):
            ring = [[self._alloc_sem(f"dma_{q}_{i}"), 0] for i in range(n_dma_sems)]
            self.dma_ring[q] = [ring, 0]
        self.n_instr = 0

    def _alloc_sem(self, name):
        cm = self.nc.semaphore(name)
        h = cm.__enter__()
        self._ctx.append(cm)
        self.sems[name] = h
        return name

    def _new_epoch(self, e):
        ep = 0 if e not in self.cur else self.cur[e][2] + 1
        self.cur[e] = [self._alloc_sem(f"s_{e}_{ep}"), 0, ep]

    def _waits(self, q, reads, writes):
        need = {}

        def add(ev):
            if ev is not None and need.get(ev[0], 0) < ev[1]:
                need[ev[0]] = ev[1]
        for b in reads:
            add(b.w)
        for b in writes:
            add(b.w)
            for ev in b.r:
                add(ev)
        out = []
        wq = self.waited[q]
        for k, v in need.items():
            if q in self.COMPUTE and k == self.cur[q][0]:
                if not self.sync_same or v > self.cur[q][1]:
                    continue
            if wq.get(k, 0) >= v:
                continue
            wq[k] = v
            out.append((k, v))
        return out

    def op(self, eng, fn, reads=(), writes=(), inc=True):
        waits = self._waits(eng, reads, writes)
        cur = self.cur[eng]
        if inc:
            cur[1] += 1
            ev = (cur[0], cur[1])
        else:
            ev = (cur[0], cur[1] + 1)
        for b in reads:
            b.r.append(ev)
            if len(b.r) > 64:
                b.r = b.r[-64:] if False else self._compact(b.r)
        for b in writes:
            b.w = ev
            b.r = []
        self.queues[eng].append((waits, fn, (cur[0], 1) if inc else None))
        self.n_instr += 1
        if inc and cur[1] >= SEM_LIMIT:
            self._new_epoch(eng)
        return ev

    @staticmethod
    def _compact(evs):
        m = {}
        for k, v in evs:
            if m.get(k, 0) < v:
                m[k] = v
        return list(m.items())

    def dma(self, q, fn, reads=(), writes=()):
        ring, idx = self.dma_ring[q]
        slot = ring[idx % len(ring)]
        self.dma_ring[q][1] = idx + 1
        waits = self._waits(q, reads, writes)
        if slot[1] > 0 and self.waited[q].get(slot[0], 0) < slot[1]:
            self.waited[q][slot[0]] = slot[1]
            waits.append((slot[0], slot[1]))
        slot[1] += 16
        ev = (slot[0], slot[1])
        for b in reads:
            b.r.append(ev)
            if len(b.r) > 64:
                b.r = self._compact(b.r)
        for b in writes:
            b.w = ev
            b.r = []
        self.queues[q].append((waits, fn, (slot[0], 16)))
        self.n_instr += 1
        return ev

    def barrier(self):
        evs = []
        for e in self.COMPUTE:
            k, c, _ = self.cur[e]
            if c > 0:
                evs.append((k, c))
        for q in self.dma_ring:
            for slot in self.dma_ring[q][0]:
                if slot[1] > 0:
                    evs.append((slot[0], slot[1]))
        for q in self.queues:
            waits = []
            for k, v in evs:
                if self.waited[q].get(k, 0) < v:
                    self.waited[q][k] = v
                    waits.append((k, v))
            if waits:
                self.queues[q].append((waits, None, None))

    def emit(self):
        nc, sems, queues = self.nc, self.sems, self.queues

        def run(engine, lst):
            for waits, fn, inc in lst:
                for k, v in waits:
                    engine.wait_ge(sems[k], v)
                if fn is None:
                    continue
                ins = fn(engine)
                if inc is not None:
                    ins.then_inc(sems[inc[0]], inc[1])

        with nc.Block() as block:
            @block.sync
            def _(sync):
                run(sync, queues["sp"])

            @block.tensor
            def _(tensor):
                run(tensor, queues["pe"])

            @block.scalar
            def _(scalar):
                run(scalar, queues["act"])

            @block.vector
            def _(vector):
                run(vector, queues["dve"])

            @block.gpsimd
            def _(gpsimd):
                run(gpsimd, queues["pool"])
        for q in queues:
            queues[q] = []


class Ctx:
    _n = 0

    def __init__(self, nc):
        self.nc = nc
        self.es = contextlib.ExitStack()

    def sb(self, shape, dt, name="t"):
        Ctx._n += 1
        return self.es.enter_context(self.nc.sbuf_tensor(f"{name}_{Ctx._n}", list(shape), dt))

    def ps(self, shape, dt, name="p"):
        Ctx._n += 1
        return self.es.enter_context(self.nc.psum_tensor(f"{name}_{Ctx._n}", list(shape), dt))

    def close(self):
        self.es.close()


def build(debug=False, stop=None, nlayers=L, skip=(), only=None, scratch_in=(), scratch_out=None):
    nc = bass.Bass("TRN2", target_bir_lowering=False)
    P = Prog(nc)
    skind = "ExternalOutput" if debug else "Internal"

    def din(name, shape, dt=F32):
        return nc.dram_tensor(name, list(shape), dt, kind="ExternalInput").ap()

    def dscr(name, shape, dt):
        if name in scratch_in:
            kind = "ExternalInput"
            used_scratch_in.append(name)
        elif debug and (scratch_out is None or name in scratch_out):
            kind = "ExternalOutput"
        else:
            kind = "Internal"
        return nc.dram_tensor(name, list(shape), dt, kind=kind).ap()

    used_scratch_in = []

    class _Lazy:
        def __init__(self, name, shape, dt=F32):
            self.name, self.shape, self.dt, self._ap = name, shape, dt, None

        def _get(self):
            if self._ap is None:
                self._ap = din(self.name, self.shape, self.dt)
                used_inputs.append(self.name)
            return self._ap

        def __getitem__(self, k):
            return self._get()[k]

        def rearrange(self, *a, **k):
            return self._get().rearrange(*a, **k)

        def partition_broadcast(self, n):
            return self._get().partition_broadcast(n)

    used_inputs = []
    x_in = _Lazy("x", [S, D])
    c_in = _Lazy("c", [128, 16])
    pos_in = _Lazy("pos", [128, NT], I32)
    consts_in = _Lazy("consts", [128, NCONST])
    w_ada = _Lazy("w_ada", [L, D, 6 * D])
    b_ada = _Lazy("b_ada", [L, 6 * D])
    norm_mix_g = _Lazy("norm_mix_g", [L, D])
    w_in = _Lazy("w_in", [L, D, IN_COLS])
    lq1 = _Lazy("lambda_q1", [L, 128])
    lk1 = _Lazy("lambda_k1", [L, 128])
    lq2 = _Lazy("lambda_q2", [L, 128])
    lk2 = _Lazy("lambda_k2", [L, 128])
    subln_g = _Lazy("diff_subln_g", [L, 256])
    w2f = _Lazy("gla_w2_fwd", [L, 16, 512])
    bfw = _Lazy("gla_b_fwd", [L, 512])
    w2b = _Lazy("gla_w2_bwd", [L, 16, 512])
    bbw = _Lazy("gla_b_bwd", [L, 512])
    gla_g = _Lazy("gla_norm_g", [L, 256])
    w_bd = _Lazy("w_branch_diff", [L, 1024, D])
    w_bg = _Lazy("w_branch_gla", [L, 1024, D])
    w_out = _Lazy("w_out", [L, D, D])
    norm_ffn_g = _Lazy("norm_ffn_g", [L, D])
    w_gate = _Lazy("w_gate", [L, D, FF])
    w_up = _Lazy("w_up", [L, D, FF])
    conv_w = _Lazy("conv_w", [L, 3, FF])
    conv_b = _Lazy("conv_b", [L, FF])
    w_down = _Lazy("w_down", [L, FF, D])
    fin_g = _Lazy("final_norm_g", [D])
    out = nc.dram_tensor("out", [S, D], F32, kind="ExternalOutput").ap()

    modrows = dscr("modrows", [L, 6, D], F32)
    qT = dscr("qT", [8, 128, S], BF16); kT = dscr("kT", [8, 128, S], BF16)
    vd = dscr("vd", [S, 1024], BF16)
    gq = dscr("gq", [S, 512], F32); gk = dscr("gk", [S, 512], F32)
    gv = dscr("gv", [S, 1024], BF16); gr = dscr("gr", [S, 1024], BF16)
    glT = dscr("glT", [32, S], F32)
    gabT = dscr("gabT", [32, 128, S], BF16)
    odT = dscr("odT", [8, 128, S], BF16)
    of_ = dscr("of", [S, 1024], F32)
    ogT = dscr("ogT", [8, 128, S], BF16)
    mT = dscr("mT", [16, 128, S], BF16)
    h2T = dscr("h2T", [16, 128, S], BF16)
    aT = dscr("aT", [NFC, 128, S], BF16)
    xa = dscr("xa", [S, D], F32)
    xb = dscr("xb", [S, D], F32)

    G = Ctx(nc)
    consts = G.sb([128, NCONST], F32, "consts"); b_consts = Buf()
    ident_bf = G.sb([128, 128], BF16, "identbf"); b_identbf = Buf()
    cosT = G.sb([128, NT, 16], F32, "cos"); sinT = G.sb([128, NT, 16], F32, "sin"); b_cs = Buf()
    ident = consts[:, C_ID:C_ID + 128]

    consts_in._get()
    P.dma("sp", lambda e: e.dma_start(out=consts[:], in_=consts_in[:, :]), writes=[b_consts])
    P.op("dve", lambda e: e.tensor_copy(ident_bf[:], ident), reads=[b_consts], writes=[b_identbf])

    def phase_rotary():
        pos_in._get()
        X = Ctx(nc)
        posi = X.sb([128, NT], I32); posf = X.sb([128, NT], F32); ang = X.sb([128, NT, 16], F32)
        r = X.sb([128, NT, 16], F32); b_t = Buf()
        P.dma("sp", lambda e: e.dma_start(out=posi[:], in_=pos_in[:, :]), writes=[b_t])
        P.op("dve", lambda e: e.tensor_copy(posf[:], posi[:]), reads=[b_t], writes=[b_t])
        invf = consts[:, C_INVF:C_INVF + 16]
        P.op("dve", lambda e: e.tensor_tensor(ang[:], posf[:].unsqueeze(2).to_broadcast([128, NT, 16]),
                                              invf.unsqueeze(1).to_broadcast([128, NT, 16]), ALU.mult),
             reads=[b_t, b_consts], writes=[b_t])
        ki = X.sb([128, NT, 16], I32); kf = X.sb([128, NT, 16], F32); mk = X.sb([128, NT, 16], F32)

        def sin_of(dst, shift):
            P.op("dve", lambda e: e.tensor_scalar_add(r[:], ang[:], shift), reads=[b_t, b_cs], writes=[b_t])
            P.op("dve", lambda e: e.tensor_scalar_mul(kf[:], r[:], 1.0 / TWO_PI), reads=[b_t], writes=[b_t])
            P.op("dve", lambda e: e.tensor_copy(ki[:], kf[:]), reads=[b_t], writes=[b_t])
            P.op("dve", lambda e: e.tensor_copy(kf[:], ki[:]), reads=[b_t], writes=[b_t])
            P.op("dve", lambda e: e.scalar_tensor_tensor(r[:], kf[:], -TWO_PI, r[:], ALU.mult, ALU.add), reads=[b_t], writes=[b_t])
            P.op("dve", lambda e: e.tensor_single_scalar(mk[:], r[:], math.pi, ALU.is_gt), reads=[b_t], writes=[b_t])
            P.op("dve", lambda e: e.scalar_tensor_tensor(r[:], mk[:], -TWO_PI, r[:], ALU.mult, ALU.add), reads=[b_t], writes=[b_t])
            P.op("dve", lambda e: e.tensor_scalar(r[:], r[:], math.pi, -math.pi, ALU.min, ALU.max), reads=[b_t], writes=[b_t])
            P.op("act", lambda e: e.activation(dst[:], r[:], AF.Sin), reads=[b_t], writes=[b_cs])
        sin_of(sinT, 0.0)
        sin_of(cosT, math.pi / 2)
        P.barrier(); P.emit(); X.close()

    def phase_adaln():
        [t._get() for t in (c_in, w_ada, b_ada, norm_mix_g, norm_ffn_g)]
        X = Ctx(nc)
        cact = X.sb([128, 16], F32); b_c = Buf()
        wa = [X.sb([128, 16, 512], BF16, "wa") for _ in range(3)]; b_wa = [Buf() for _ in range(3)]
        cbf = X.sb([128, 16], BF16, "cbf")
        row = X.sb([1, 6 * D], F32, "row"); b_row = Buf()
        brow = X.sb([1, 6 * D], F32, "brow"); b_brow = Buf()
        grow = X.sb([1, 2 * D], F32, "grow"); b_grow = Buf()
        pacc = [X.ps([1, 512], F32, "pacc") for _ in range(2)]; b_pacc = [Buf(), Buf()]
        P.dma("sp", lambda e: e.dma_start(out=cact[:], in_=c_in[:, :]), writes=[b_c])
        P.op("act", lambda e: e.activation(cact[:], cact[:], AF.Silu), reads=[b_c], writes=[b_c])
        P.op("dve", lambda e: e.tensor_copy(cbf[:], cact[:]), reads=[b_c], writes=[b_c])
        it = 0
        for l in range(L):
            P.dma("sp", lambda e, l=l: e.dma_start(out=brow[:], in_=b_ada[l:l + 1, :]), writes=[b_brow])
            P.dma("sp", lambda e, l=l: e.dma_start(out=grow[:, 0:D], in_=norm_mix_g[l:l + 1, :]), writes=[b_grow])
            P.dma("sp", lambda e, l=l: e.dma_start(out=grow[:, D:2 * D], in_=norm_ffn_g[l:l + 1, :]), writes=[b_grow])
            for nb in range(24):
                i = it % 3; pi = it % 2; it += 1
                src = w_ada[l][:, nb * 512:(nb + 1) * 512].rearrange("(p j) n -> p j n", j=16)
                P.dma("pool", lambda e, i=i, src=src: e.dma_start(out=wa[i][:], in_=src), writes=[b_wa[i]])
                for j in range(16):
                    P.op("pe", lambda e, i=i, pi=pi, j=j: e.matmul(pacc[pi][:], cbf[:, j:j + 1], wa[i][:, j, :],
                                                                    start=(j == 0), stop=(j == 15)),
                         reads=[b_c, b_wa[i]], writes=[b_pacc[pi]], inc=(j == 15))
                P.op("dve", lambda e, pi=pi, nb=nb: e.tensor_add(row[:, nb * 512:(nb + 1) * 512], pacc[pi][:],
                                                                brow[:, nb * 512:(nb + 1) * 512]),
                     reads=[b_pacc[pi], b_brow], writes=[b_row])
            P.op("dve", lambda e: e.scalar_tensor_tensor(row[:, D:2 * D], row[:, D:2 * D], 1.0, grow[:, 0:D], ALU.add, ALU.mult),
                 reads=[b_row, b_grow], writes=[b_row])
            P.op("dve", lambda e: e.scalar_tensor_tensor(row[:, 4 * D:5 * D], row[:, 4 * D:5 * D], 1.0, grow[:, D:2 * D], ALU.add, ALU.mult),
                 reads=[b_row, b_grow], writes=[b_row])
            for dst, srcc in enumerate([1, 0, 2, 4, 3, 5]):
                P.dma("sp", lambda e, l=l, dst=dst, srcc=srcc: e.dma_start(out=modrows[l, dst:dst + 1, :], in_=row[:, srcc * D:(srcc + 1) * D]),
                      reads=[b_row])
        P.barrier(); P.emit(); X.close()

    def bcast_load(q, tile, b_tile, row_ap):
        P.dma(q, lambda e: e.dma_start(out=tile, in_=row_ap.partition_broadcast(128)), writes=[b_tile])

    def phase_inproj(l, xsrc):
        [t._get() for t in (x_in, w_in)]
        X = Ctx(nc)
        hT = X.sb([128, KC, S], BF16, "hT"); b_hT = [Buf() for _ in range(NT)]
        Y = Ctx(nc)
        Ab = Y.sb([128, D], F32, "Ab"); Bb = Y.sb([128, D], F32, "Bb"); b_ab = Buf()
        bcast_load("sp", Ab[:], b_ab, modrows[l, 0])
        bcast_load("sp", Bb[:], b_ab, modrows[l, 1])
        xt = [Y.sb([128, D], F32, "xt") for _ in range(2)]; b_xt = [Buf(), Buf()]
        hbf = [Y.sb([128, D], BF16, "hbf") for _ in range(2)]; b_hbf = [Buf(), Buf()]
        junk = Y.sb([128, D], BF16, "junk"); ss = [Y.sb([128, 4], F32, "ss") for _ in range(2)]; b_tmp = [Buf(), Buf()]
        tp = [Y.ps([128, KC, 128], BF16, "tp") for _ in range(2)]; b_tp = [Buf(), Buf()]
        pend1 = []
        for t in range(NT):
            i = t % 2
            P.dma("sp", lambda e, i=i, t=t: e.dma_start(out=xt[i][:], in_=xsrc[t * 128:(t + 1) * 128, :]), writes=[b_xt[i]])
            rms_mod(xt[i], b_xt[i], Ab, Bb, b_ab, hbf[i], b_hbf[i], ss[i], junk, b_tmp[i])

            def later(i=i, t=t):
                for kc in range(KC):
                    P.op("pe", lambda e, kc=kc: e.transpose(tp[i][:, kc, :], hbf[i][:, kc * 128:(kc + 1) * 128], ident_bf[:]),
                         reads=[b_hbf[i], b_identbf], writes=[b_tp[i]], inc=(kc == KC - 1))
                P.op("act", lambda e: e.copy(hT[:, :, t * 128:(t + 1) * 128], tp[i][:]), reads=[b_tp[i]], writes=[b_hT[t]])
            pend1.append(later)
            if len(pend1) > 1:
                pend1.pop(0)()
        for f in pend1:
            f()
        P.barrier(); P.emit(); Y.close()
        if stop == f"inproj{l}a":
            X.close()
            return
        Y = Ctx(nc)
        wt = [Y.sb([128, KC, 512], BF16, "wt") for _ in range(2)]; b_wt = [[Buf() for _ in range(KC)] for _ in range(2)]
        acc = [Y.ps([128, 512], F32, "acc") for _ in range(3)]; b_acc = [Buf() for _ in range(3)]
        tq = [Y.ps([128, 4, 128], BF16, "tq") for _ in range(2)]; b_tq = [Buf(), Buf()]
        stg_bf = [Y.sb([128, 512], BF16, "stgb") for _ in range(3)]; b_sbf = [Buf() for _ in range(3)]
        stg_f = [Y.sb([128, 512], F32, "stgf") for _ in range(3)]; b_sf = [Buf() for _ in range(3)]
        stg_T = [Y.sb([128, 4, 128], BF16, "stgT") for _ in range(2)]; b_sT = [Buf(), Buf()]
        rt = [Y.sb([128, 4, 4, 16], F32, "rt") for _ in range(2)]; b_rt = [Buf(), Buf()]
        wsm = Y.sb([128, KC, 32], BF16, "wsm"); b_wsm = [Buf() for _ in range(KC)]
        cnt = {"acc": 0, "sbf": 0, "sf": 0, "tq": 0, "w": 0}

        def load_w(c0, n, tile, b):
            for kc in range(KC):
                src = w_in[l][kc * 128:(kc + 1) * 128, c0:c0 + n]
                P.dma("pool", lambda e, kc=kc, src=src: e.dma_start(out=tile[:, kc, :], in_=src), writes=[b[kc]])

        def tok_major(c0, handler, defer=0):
            wi = cnt["w"] % 2; cnt["w"] += 1
            load_w(c0, 512, wt[wi][:], b_wt[wi])
            pend = []
            for t in range(NT):
                ai = cnt["acc"] % 3; cnt["acc"] += 1
                for kc in range(KC):
                    P.op("pe", lambda e, ai=ai, kc=kc, t=t, wi=wi: e.matmul(acc[ai][:], hT[:, kc, t * 128:(t + 1) * 128], wt[wi][:, kc, :],
                                                                         start=(kc == 0), stop=(kc == KC - 1)),
                         reads=[b_hT[t], b_wt[wi][kc]], writes=[b_acc[ai]], inc=(kc == KC - 1))
                later = handler(t, acc[ai], b_acc[ai])
                if later is not None:
                    pend.append(later)
                    if len(pend) > defer:
                        pend.pop(0)()
            for f in pend:
                f()

        def h_store(dst, col0, f32=False, func=None):
            def h(t, a, b_a):
                if f32:
                    si = cnt["sf"] % 3; cnt["sf"] += 1
                    st, bs = stg_f[si], b_sf[si]
                else:
                    si = cnt["sbf"] % 3; cnt["sbf"] += 1
                    st, bs = stg_bf[si], b_sbf[si]
                if func is None:
                    P.op("act", lambda e: e.copy(st[:], a[:]), reads=[b_a], writes=[bs])
                else:
                    P.op("act", lambda e: e.activation(st[:], a[:], func), reads=[b_a], writes=[bs])
                P.dma("sp", lambda e: e.dma_start(out=dst[t * 128:(t + 1) * 128, col0:col0 + 512], in_=st[:]), reads=[bs])
            return h

        def h_rot(dstT, hm0):
            def h(t, a, b_a):
                si = cnt["sbf"] % 3; cnt["sbf"] += 1
                st, bs = stg_bf[si], b_sbf[si]
                ri = cnt["tq"] % 2; cnt["tq"] += 1
                fi = cnt["sf"] % 3; cnt["sf"] += 1
                s32, b32 = stg_f[fi], b_sf[fi]
                P.op("act", lambda e: e.copy(s32[:], a[:]), reads=[b_a], writes=[b32])
                R = rt[ri]
                s4 = s32[:].rearrange("p (h d) -> p h d", h=4)
                x1 = s4[:, :, 0:16]; x2 = s4[:, :, 16:32]
                cb = cosT[:, t, :].unsqueeze(1).to_broadcast([128, 4, 16])
                sn = sinT[:, t, :].unsqueeze(1).to_broadcast([128, 4, 16])
                P.op("dve", lambda e: e.tensor_tensor(R[:, 0], x1, cb, ALU.mult), reads=[b32, b_cs], writes=[b_rt[ri]])
                P.op("dve", lambda e: e.tensor_tensor(R[:, 1], x2, sn, ALU.mult), reads=[b32, b_cs], writes=[b_rt[ri]])
                P.op("dve", lambda e: e.tensor_tensor(R[:, 2], x2, cb, ALU.mult), reads=[b32, b_cs], writes=[b_rt[ri]])
                P.op("dve", lambda e: e.tensor_tensor(R[:, 3], x1, sn, ALU.mult), reads=[b32, b_cs], writes=[b_rt[ri]])
                P.op("dve", lambda e: e.tensor_tensor(x1, R[:, 0], R[:, 1], ALU.subtract), reads=[b_rt[ri]], writes=[b32])
                P.op("dve", lambda e: e.tensor_tensor(x2, R[:, 2], R[:, 3], ALU.add), reads=[b_rt[ri]], writes=[b32])
                P.op("dve", lambda e: e.tensor_copy(st[:], s32[:]), reads=[b32], writes=[bs])

                def later():
                    for hh in range(4):
                        P.op("pe", lambda e, hh=hh: e.transpose(tq[ri][:, hh, :], st[:, hh * 128:(hh + 1) * 128], ident_bf[:]),
                             reads=[bs, b_identbf], writes=[b_tq[ri]], inc=(hh == 3))
                    P.op("act", lambda e: e.copy(stg_T[ri][:], tq[ri][:]), reads=[b_tq[ri]], writes=[b_sT[ri]])
                    P.dma("sp", lambda e: e.dma_start(out=dstT[hm0:hm0 + 4, :, t * 128:(t + 1) * 128].rearrange("h p n -> p h n"), in_=stg_T[ri][:]),
                          reads=[b_sT[ri]])
                return later
            return h

        tok_major(0, h_rot(qT, 0), defer=1)
        if stop == f"inproj{l}b":
            P.barrier(); P.emit(); Y.close(); X.close()
            return
        tok_major(512, h_rot(qT, 4), defer=1)
        tok_major(1024, h_rot(kT, 0), defer=1); tok_major(1536, h_rot(kT, 4), defer=1)
        tok_major(2048, h_store(vd, 0)); tok_major(2560, h_store(vd, 512))
        tok_major(3072, h_store(gq, 0, True)); tok_major(3584, h_store(gk, 0, True))
        tok_major(4096, h_store(gv, 0)); tok_major(4608, h_store(gv, 512))
        tok_major(5120, h_store(gr, 0, func=AF.Silu)); tok_major(5632, h_store(gr, 512, func=AF.Silu))
        load_w(6144, 32, wsm[:], b_wsm)
        for tb in range(8):
            ai = cnt["acc"] % 3; cnt["acc"] += 1
            for kc in range(KC):
                P.op("pe", lambda e, ai=ai, kc=kc, tb=tb: e.matmul(acc[ai][0:32, :], wsm[:, kc, :], hT[:, kc, tb * 512:(tb + 1) * 512],
                                                                  start=(kc == 0), stop=(kc == KC - 1)),
                     reads=[b_wsm[kc]] + b_hT[tb * 4:(tb + 1) * 4], writes=[b_acc[ai]], inc=(kc == KC - 1))
            si = cnt["sf"] % 3; cnt["sf"] += 1
            P.op("act", lambda e, si=si, ai=ai: e.copy(stg_f[si][0:32, :], acc[ai][0:32, :]), reads=[b_acc[ai]], writes=[b_sf[si]])
            P.dma("sp", lambda e, si=si, tb=tb: e.dma_start(out=glT[:, tb * 512:(tb + 1) * 512], in_=stg_f[si][0:32, :]), reads=[b_sf[si]])
        for gg in range(8):
            wi = cnt["w"] % 2; cnt["w"] += 1
            load_w(6176 + gg * 512, 512, wt[wi][:], b_wt[wi])
            for j in range(4):
                ch = gg * 4 + j
                for tb in range(8):
                    ai = cnt["acc"] % 3; cnt["acc"] += 1
                    for kc in range(KC):
                        P.op("pe", lambda e, ai=ai, kc=kc, tb=tb, wi=wi, j=j: e.matmul(acc[ai][:], wt[wi][:, kc, j * 128:(j + 1) * 128],
                                                                                     hT[:, kc, tb * 512:(tb + 1) * 512],
                                                                                     start=(kc == 0), stop=(kc == KC - 1)),
                             reads=[b_wt[wi][kc]] + b_hT[tb * 4:(tb + 1) * 4], writes=[b_acc[ai]], inc=(kc == KC - 1))
                    si = cnt["sbf"] % 3; cnt["sbf"] += 1
                    eng = "act" if (tb % 2 == 0) else "dve"
                    if eng == "act":
                        P.op("act", lambda e, si=si, ai=ai: e.copy(stg_bf[si][:], acc[ai][:]), reads=[b_acc[ai]], writes=[b_sbf[si]])
                    else:
                        P.op("dve", lambda e, si=si, ai=ai: e.tensor_copy(stg_bf[si][:], acc[ai][:]), reads=[b_acc[ai]], writes=[b_sbf[si]])
                    P.dma("sp", lambda e, si=si, ch=ch, tb=tb: e.dma_start(out=gabT[ch, :, tb * 512:(tb + 1) * 512], in_=stg_bf[si][:]),
                          reads=[b_sbf[si]])
        P.barrier(); P.emit(); Y.close(); X.close()

    def rms_mod(xt, b_xt, Ab, Bb, b_ab, hbf, b_hbf, ss, junk, b_tmp, dwidth=D):
        P.op("act", lambda e: e.activation(junk[:], xt[:], AF.Square, accum_out=ss[:, 0:1]), reads=[b_xt], writes=[b_tmp])
        P.op("dve", lambda e: e.tensor_scalar(ss[:, 1:2], ss[:, 0:1], 1.0 / dwidth, EPS, ALU.mult, ALU.add), reads=[b_tmp], writes=[b_tmp])
        P.op("act", lambda e: e.activation(ss[:, 1:2], ss[:, 1:2], AF.Sqrt), reads=[b_tmp], writes=[b_tmp])
        P.op("dve", lambda e: e.reciprocal(ss[:, 2:3], ss[:, 1:2]), reads=[b_tmp], writes=[b_tmp])
        P.op("dve", lambda e: e.scalar_tensor_tensor(xt[:], xt[:], ss[:, 2:3], Ab[:], ALU.mult, ALU.mult),
             reads=[b_xt, b_tmp, b_ab], writes=[b_xt])
        if Bb is not None:
            P.op("dve", lambda e: e.tensor_add(hbf[:], xt[:], Bb[:]), reads=[b_xt, b_ab], writes=[b_hbf])

    def phase_diffattn(l):
        [t._get() for t in (lq1, lk1, lq2, lk2, subln_g)]
        X = Ctx(nc)
        lam_init = 0.8 - 0.6 * math.exp(-0.3 * l)
        scale = 128 ** -0.5
        lt = X.sb([128, 4, 128], F32, "lt"); b_lt = Buf()
        lv = X.sb([128, 8], F32, "lv"); b_lv = Buf()
        for i, src in enumerate([lq1, lk1, lq2, lk2]):
            bcast_load("sp", lt[:, i, :], b_lt, src[l])
        P.op("dve", lambda e: e.tensor_tensor(lt[:, 0, :], lt[:, 0, :], lt[:, 1, :], ALU.mult), reads=[b_lt], writes=[b_lt])
        P.op("dve", lambda e: e.tensor_tensor(lt[:, 2, :], lt[:, 2, :], lt[:, 3, :], ALU.mult), reads=[b_lt], writes=[b_lt])
        P.op("dve", lambda e: e.reduce_sum(lv[:, 0:1], lt[:, 0, :], AX.X), reads=[b_lt], writes=[b_lv])
        P.op("dve", lambda e: e.reduce_sum(lv[:, 1:2], lt[:, 2, :], AX.X), reads=[b_lt], writes=[b_lv])
        P.op("act", lambda e: e.activation(lv[:, 2:4], lv[:, 0:2], AF.Exp), reads=[b_lv], writes=[b_lv])
        P.op("dve", lambda e: e.tensor_tensor(lv[:, 4:5], lv[:, 3:4], lv[:, 2:3], ALU.subtract), reads=[b_lv], writes=[b_lv])
        P.op("dve", lambda e: e.tensor_scalar_add(lv[:, 5:6], lv[:, 4:5], -lam_init), reads=[b_lv], writes=[b_lv])
        nlam = lv[:, 5:6]
        sg = X.sb([128, 256], F32, "sg"); b_sg = Buf()
        bcast_load("sp", sg[:], b_sg, subln_g[l])
        P.op("dve", lambda e: e.tensor_scalar_mul(sg[:], sg[:], 1.0 - lam_init), reads=[b_sg], writes=[b_sg])

        vaug = [X.sb([128, NT, 257], BF16, "vaug") for _ in range(2)]; b_v = [Buf(), Buf()]
        kt = [X.sb([128, S], BF16, "kt") for _ in range(4)]; b_kt = [Buf() for _ in range(4)]
        qb_t = [X.sb([128, 512], BF16, "qb") for _ in range(4)]; b_qb = [Buf() for _ in range(4)]
        pT = [X.sb([128, 512], BF16, "pT") for _ in range(3)]; b_pT = [Buf() for _ in range(3)]
        osb = [X.sb([128, 4, 257], F32, "osb") for _ in range(2)]; b_osb = [Buf(), Buf()]
        o1 = X.sb([128, 4, 256], F32, "o1"); b_o1 = Buf()
        od4 = X.sb([128, 4, 256], F32, "od4"); b_od = Buf()
        sq4 = X.sb([128, 4, 256], F32, "sq4"); b_sq = Buf()
        odb4 = X.sb([128, 4, 256], BF16, "odb4"); b_odb = Buf()
        rs = [X.sb([128, 4, 1], F32, "rs") for _ in range(2)]; b_rs = [Buf(), Buf()]
        st4 = X.sb([128, 4, 4], F32, "st4"); b_st = Buf()
        stg = [X.sb([128, 2, 512], BF16, "stg") for _ in range(2)]; b_stg = [Buf(), Buf()]
        sps = [X.ps([128, 512], F32, "sps") for _ in range(3)]; b_sps = [Buf() for _ in range(3)]
        ops = X.ps([128, 4, 512], F32, "ops"); b_ops = [Buf() for _ in range(4)]
        tps = X.ps([128, 8, 128], BF16, "tps"); b_tps = Buf()
        for i in range(2):
            P.op("dve", lambda e, i=i: e.memset(vaug[i][:, :, 256:257], 1.0), writes=[b_v[i]])
        n_p = 0; n_q = 0; n_blk = 0
        pend = {1: [], 6: [], 12: []}

        def flush(kc):
            for f in pend[kc]:
                f()
            pend[kc] = []

        def epilogue(m, ob, b_ob, rsx, b_rsx, h, qb, si):
            bc = lambda ap: ap.to_broadcast([128, 4, 256])
            def stage_a():
                P.op("dve", lambda e: e.reciprocal(rsx[:], ob[:, :, 256:257]), reads=[b_ob], writes=[b_rsx])
                if m == 0:
                    P.op("dve", lambda e: e.tensor_tensor(o1[:], ob[:, :, 0:256], bc(rsx[:]), ALU.mult), reads=[b_ob, b_rsx], writes=[b_o1])
                    return
                P.op("dve", lambda e: e.tensor_scalar_mul(st4[:, :, 0:1], rsx[:], nlam), reads=[b_rsx, b_lv], writes=[b_st])
                P.op("dve", lambda e: e.tensor_tensor(od4[:], ob[:, :, 0:256], bc(st4[:, :, 0:1]), ALU.mult), reads=[b_ob, b_st], writes=[b_od])
                P.op("dve", lambda e: e.tensor_tensor(od4[:], od4[:], o1[:], ALU.add), reads=[b_od, b_o1], writes=[b_od])
                P.op("dve", lambda e: e.tensor_tensor(sq4[:], od4[:], od4[:], ALU.mult), reads=[b_od], writes=[b_sq])
                P.op("dve", lambda e: e.reduce_sum(st4[:, :, 1], sq4[:], AX.X), reads=[b_sq], writes=[b_st])
                P.op("dve", lambda e: e.tensor_scalar(st4[:, :, 1], st4[:, :, 1], 1.0 / 256, EPS, ALU.mult, ALU.add), reads=[b_st], writes=[b_st])
            def stage_b():
                if m == 0:
                    return
                P.op("act", lambda e: e.activation(st4[:, :, 2], st4[:, :, 1], AF.Ln), reads=[b_st], writes=[b_st])
                P.op("act", lambda e: e.activation(st4[:, :, 3], st4[:, :, 2], AF.Exp, scale=-0.5), reads=[b_st], writes=[b_st])
                P.op("dve", lambda e: e.tensor_tensor(od4[:], od4[:], bc(st4[:, :, 3:4]), ALU.mult), reads=[b_od, b_st], writes=[b_od])
                P.op("dve", lambda e: e.tensor_tensor(odb4[:], od4[:], sg[:].unsqueeze(1).to_broadcast([128, 4, 256]), ALU.mult),
                     reads=[b_od, b_sg], writes=[b_odb])
            def stage_c():
                if m == 0:
                    return
                for hf in range(2):
                    for s in range(4):
                        P.op("pe", lambda e, hf=hf, s=s: e.transpose(tps[:, hf * 4 + s, :], odb4[:, s, hf * 128:(hf + 1) * 128], ident_bf[:]),
                             reads=[b_odb, b_identbf], writes=[b_tps], inc=(hf == 1 and s == 3))
                P.op("act", lambda e: e.copy(stg[si][:], tps[:].rearrange("p (a b) c -> p a (b c)", a=2)), reads=[b_tps], writes=[b_stg[si]])
                P.dma("sp", lambda e: e.dma_start(out=odT[h * 2:h * 2 + 2, :, qb * 512:(qb + 1) * 512].rearrange("c p n -> p c n"),
                                                 in_=stg[si][:]), reads=[b_stg[si]])
            pend[1].append(stage_a); pend[6].append(stage_b); pend[12].append(stage_c)

        for h in range(4):
            vi = h % 2
            P.dma("sp", lambda e, h=h, vi=vi: e.dma_start(out=vaug[vi][:, :, 0:256],
                                                           in_=vd[:, h * 256:(h + 1) * 256].rearrange("(kc p) e -> p kc e", p=128)),
                  writes=[b_v[vi]])
            for m in range(2):
                ki = (h * 2 + m) % 4
                P.dma("sp", lambda e, h=h, m=m, ki=ki: e.dma_start(out=kt[ki][:], in_=kT[h * 2 + m]), writes=[b_kt[ki]])
            for qb in range(8):
                si = (h * 8 + qb) % 2
                for m in range(2):
                    ki = (h * 2 + m) % 4
                    qi = n_q % 4; n_q += 1
                    P.dma("sp", lambda e, h=h, m=m, qb=qb, qi=qi: e.dma_start(out=qb_t[qi][:], in_=qT[h * 2 + m, :, qb * 512:(qb + 1) * 512]),
                          writes=[b_qb[qi]])

                    def emit_qk(kc, ki=ki, qi=qi):
                        s_i = kc % 3
                        P.op("pe", lambda e, s_i=s_i, ki=ki, kc=kc, qi=qi: e.matmul(sps[s_i][:], kt[ki][:, kc * 128:(kc + 1) * 128], qb_t[qi][:],
                                                                                  start=True, stop=True),
                             reads=[b_kt[ki], b_qb[qi]], writes=[b_sps[s_i]])
                    emit_qk(0); emit_qk(1)
                    for kc in range(NT):
                        if kc + 2 < NT:
                            emit_qk(kc + 2)
                        s_i = kc % 3
                        p_i = n_p % 3; n_p += 1
                        P.op("act", lambda e, s_i=s_i, p_i=p_i: e.activation(pT[p_i][:], sps[s_i][:], AF.Exp, scale=scale),
                             reads=[b_sps[s_i]], writes=[b_pT[p_i]])
                        for s in range(4):
                            P.op("pe", lambda e, s=s, p_i=p_i, kc=kc, vi=vi: e.matmul(ops[:, s, 0:257], pT[p_i][:, s * 128:(s + 1) * 128], vaug[vi][:, kc, :],
                                                                                   start=(kc == 0), stop=(kc == NT - 1)),
                                 reads=[b_pT[p_i], b_v[vi]], writes=[b_ops[s]], inc=(s == 3))
                        if kc in pend:
                            flush(kc)
                    oi = n_blk % 2; n_blk += 1
                    for s in range(4):
                        P.op("act", lambda e, s=s, oi=oi: e.copy(osb[oi][:, s, :], ops[:, s, 0:257]), reads=[b_ops[s]], writes=[b_osb[oi]])
                    epilogue(m, osb[oi], b_osb[oi], rs[oi], b_rs[oi], h, qb, si)
        for kc in (1, 6, 12):
            flush(kc)
        P.barrier(); P.emit(); X.close()

    def phase_gla(l):
        [t._get() for t in (w2f, bfw, w2b, bbw, gla_g)]
        X = Ctx(nc)
        ball1 = X.sb([128, NT, 512], F32, "ball"); blast = X.sb([128, NT, 4], F32, "blast"); b_blast = [Buf() for _ in range(NT)]
        ball = [ball1, ball1, blast, b_blast]; b_ball = [[Buf() for _ in range(NT)] for _ in range(2)]
        for d in range(2):
            gla_decay(l, d, ball, b_ball)
            gla_scan(l, d, ball, b_ball)
        X.close()

    def gla_decay(l, d, ball, b_ball):
        Y = Ctx(nc)
        blast, b_blast = ball[2], ball[3]
        glt32 = Y.sb([16, S], F32, "glt32"); glt = Y.sb([16, S], BF16, "glt"); b_glt = Buf()
        w232 = Y.sb([16, 512], F32, "w232"); w2 = Y.sb([16, 512], BF16, "w2"); b_w2 = Buf()
        bias = Y.sb([128, 512], F32, "bias"); b_bias = Buf()
        tri = Y.sb([128, 128], BF16, "tri"); b_tri = Buf()
        wsrc, bsrc = (w2f, bfw) if d == 0 else (w2b, bbw)
        P.dma("sp", lambda e: e.dma_start(out=glt32[:], in_=glT[d * 16:(d + 1) * 16, :]), writes=[b_glt])
        P.op("dve", lambda e: e.tensor_copy(glt[:], glt32[:]), reads=[b_glt], writes=[b_glt])
        P.dma("sp", lambda e: e.dma_start(out=w232[:], in_=wsrc[l]), writes=[b_w2])
        P.op("dve", lambda e: e.tensor_copy(w2[:], w232[:]), reads=[b_w2], writes=[b_w2])
        bcast_load("sp", bias[:], b_bias, bsrc[l])
        tsrc = consts[:, C_TRIF:C_TRIF + 128] if d == 0 else consts[:, C_TRIB:C_TRIB + 128]
        P.op("dve", lambda e: e.tensor_copy(tri[:], tsrc), reads=[b_consts], writes=[b_tri])
        negblk = Y.sb([128, 8], BF16, "negblk")
        P.op("dve", lambda e: e.memset(negblk[:], -1.0 / 16.0), writes=[b_tri])
        zps = [Y.ps([128, 512], F32, "zps") for _ in range(2)]; b_zps = [Buf(), Buf()]
        bps = [Y.ps([128, 512], F32, "bps") for _ in range(2)]; b_bps = [Buf(), Buf()]
        lps = [Y.ps([128, 4, 8], F32, "lps") for _ in range(2)]; b_lps = [Buf(), Buf()]
        zs = [Y.sb([128, 512], F32, "zs") for _ in range(2)]; b_zs = [Buf(), Buf()]
        hi = [Y.sb([128, 512], BF16, "hi") for _ in range(2)]; lo = [Y.sb([128, 512], BF16, "lo") for _ in range(2)]
        b_hl = [Buf(), Buf()]
        def stage1(ci):
            i = ci % 2
            P.op("pe", lambda e: e.matmul(zps[i][:], glt[:, ci * 128:(ci + 1) * 128], w2[:], start=True, stop=True),
                 reads=[b_glt, b_w2], writes=[b_zps[i]])
            P.op("dve", lambda e: e.tensor_add(zs[i][:], zps[i][:], bias[:]), reads=[b_zps[i], b_bias], writes=[b_zs[i]])
            P.op("act", lambda e: e.activation(zs[i][:], zs[i][:], AF.Exp, scale=-1.0), reads=[b_zs[i]], writes=[b_zs[i]])
            P.op("act", lambda e: e.activation(zs[i][:], zs[i][:], AF.Ln, bias=1.0), reads=[b_zs[i]], writes=[b_zs[i]])
            P.op("dve", lambda e: e.tensor_copy(hi[i][:], zs[i][:]), reads=[b_zs[i]], writes=[b_hl[i]])
            P.op("dve", lambda e: e.tensor_tensor(lo[i][:], zs[i][:], hi[i][:], ALU.subtract), reads=[b_zs[i], b_hl[i]], writes=[b_hl[i]])

        def stage2(ci):
            i = ci % 2
            P.op("pe", lambda e: e.matmul(bps[i][:], tri[:], hi[i][:], start=True, stop=False), reads=[b_tri, b_hl[i]], writes=[b_bps[i]], inc=False)
            P.op("pe", lambda e: e.matmul(bps[i][:], tri[:], lo[i][:], start=False, stop=True), reads=[b_tri, b_hl[i]], writes=[b_bps[i]])
            for hh in range(4):
                P.op("pe", lambda e, hh=hh: e.matmul(lps[i][:, hh, :], hi[i][:, hh * 128:(hh + 1) * 128], negblk[:], start=True, stop=False),
                     reads=[b_tri, b_hl[i]], writes=[b_lps[i]], inc=False)
                P.op("pe", lambda e, hh=hh: e.matmul(lps[i][:, hh, :], lo[i][:, hh * 128:(hh + 1) * 128], negblk[:], start=False, stop=True),
                     reads=[b_tri, b_hl[i]], writes=[b_lps[i]], inc=(hh == 3))
            P.op("dve", lambda e: e.tensor_copy(ball[d][:, ci, :], bps[i][:]), reads=[b_bps[i]], writes=[b_ball[d][ci]])
            P.op("act", lambda e: e.copy(blast[:, ci, :], lps[i][:, :, 0]), reads=[b_lps[i]], writes=[b_blast[ci]])

        stage1(0)
        for ci in range(NT):
            if ci + 1 < NT:
                stage1(ci + 1)
            stage2(ci)
        P.barrier(); P.emit(); Y.close()

    def gla_scan(l, d, ball, b_ball):
        Y = Ctx(nc)
        NB = 2
        gq_t = [Y.sb([128, 512], F32, "gq") for _ in range(NB)]; gk_t = [Y.sb([128, 512], F32, "gk") for _ in range(NB)]
        gv_t = [Y.sb([128, 1024], BF16, "gv") for _ in range(NB)]; b_in = [Buf() for _ in range(NB)]
        eb = [Y.sb([128, 512], F32, "eb") for _ in range(NB)]; enb = [Y.sb([128, 512], F32, "enb") for _ in range(NB)]
        b_eb = [Buf() for _ in range(NB)]; b_enb = [Buf() for _ in range(NB)]
        qk = [Y.sb([128, 2, 512], BF16, "qk") for _ in range(NB)]; b_qk = [Buf() for _ in range(NB)]
        qkT = [Y.sb([128, 8, 128], BF16, "qkT") for _ in range(NB)]; b_qkT = [Buf() for _ in range(NB)]
        sT = [Y.sb([128, 4, 128], BF16, "sT") for _ in range(NB)]; b_sT = [Buf() for _ in range(NB)]
        ebl = [Y.sb([128, 4], F32, "ebl") for _ in range(NB)]; b_ebl = [Buf() for _ in range(NB)]
        St = Y.sb([128, 4, 256], F32, "St"); Sbf = Y.sb([128, 4, 256], BF16, "Sbf"); b_S = Buf(); b_Sbf = Buf()
        ostg = [Y.sb([128, 1024], F32, "ostg") for _ in range(NB)]; b_ostg = [Buf() for _ in range(NB)]
        oprev = [Y.sb([128, 1024], F32, "oprev") for _ in range(NB)]; b_oprev = [Buf() for _ in range(NB)]
        grt = [Y.sb([128, 1024], BF16, "grt") for _ in range(3)]; b_grt = [Buf() for _ in range(3)]
        sqt = Y.sb([128, 1024], F32, "sqt"); b_sqt = Buf()
        ogb = [Y.sb([128, 1024], BF16, "ogb") for _ in range(NB)]; b_ogb = [Buf() for _ in range(NB)]
        ogs = [Y.sb([128, 8, 128], BF16, "ogs") for _ in range(NB)]; b_ogs = [Buf() for _ in range(NB)]
        gg = Y.sb([128, 4, 256], F32, "gg"); b_gg = Buf()
        junk = Y.sb([128, 256], BF16, "junk")
        ssn = [Y.sb([128, 16], F32, "ssn") for _ in range(NB)]; b_ssn = [Buf() for _ in range(NB)]
        for hh in range(4):
            bcast_load("sp", gg[:, hh, :], b_gg, gla_g[l])
        tps = Y.ps([128, 8, 128], BF16, "tps"); b_tps = Buf()
        sps = Y.ps([128, 4, 128], F32, "sps"); b_sps = Buf()
        ops = Y.ps([128, 4, 256], F32, "ops"); b_ops = Buf()
        ups = Y.ps([128, 4, 256], F32, "ups"); b_ups = Buf()
        gps = Y.ps([128, 8, 128], BF16, "gps"); b_gps = Buf()
        ssb = Y.sb([128, 4, 128], F32, "ssb"); b_ssb = Buf()
        Sb2 = [Sbf, Y.sb([128, 4, 256], BF16, "Sbf2")]; b_Sb2 = [b_Sbf, Buf()]
        P.op("dve", lambda e: e.memset(St[:], 0.0), writes=[b_S])
        P.op("dve", lambda e: e.memset(Sb2[0][:], 0.0), writes=[b_Sb2[0]])
        mask = consts[:, C_MF:C_MF + 128] if d == 0 else consts[:, C_MB:C_MB + 128]

        def stage_a(cc):
            ci = cc if d == 0 else NT - 1 - cc
            i = cc % NB
            tok = slice(ci * 128, (ci + 1) * 128)
            P.dma("sp", lambda e: e.dma_start(out=gq_t[i][:], in_=gq[tok, :]), writes=[b_in[i]])
            P.dma("sp", lambda e: e.dma_start(out=gk_t[i][:], in_=gk[tok, :]), writes=[b_in[i]])
            P.dma("sp", lambda e: e.dma_start(out=gv_t[i][:], in_=gv[tok, :]), writes=[b_in[i]])
            if d == 1:
                P.dma("sp", lambda e: e.dma_start(out=oprev[i][:], in_=of_[tok, :]), writes=[b_oprev[i]])
                P.dma("sp", lambda e: e.dma_start(out=grt[cc % 3][:], in_=gr[tok, :]), writes=[b_grt[cc % 3]])
            bsrc = ball[d][:, ci, :]
            P.op("act", lambda e: e.activation(eb[i][:], bsrc, AF.Exp), reads=[b_ball[d][ci]], writes=[b_eb[i]])
            P.op("act", lambda e: e.activation(enb[i][:], bsrc, AF.Exp, scale=-1.0), reads=[b_ball[d][ci]], writes=[b_enb[i]])
            P.op("act", lambda e: e.activation(ebl[i][:], ball[2][:, ci, :], AF.Exp), reads=[ball[3][ci]], writes=[b_ebl[i]])
            P.op("dve", lambda e: e.scalar_tensor_tensor(qk[i][:, 0, :], gq_t[i][:], 128 ** -0.5, eb[i][:], ALU.mult, ALU.mult),
                 reads=[b_in[i], b_eb[i]], writes=[b_qk[i]])
            P.op("dve", lambda e: e.tensor_tensor(qk[i][:, 1, :], gk_t[i][:], enb[i][:], ALU.mult),
                 reads=[b_in[i], b_enb[i]], writes=[b_qk[i]])
            for j in range(8):
                P.op("pe", lambda e, j=j: e.transpose(tps[:, j, :], qk[i][:, j // 4, (j % 4) * 128:(j % 4 + 1) * 128], ident_bf[:]),
                     reads=[b_qk[i], b_identbf], writes=[b_tps], inc=(j == 7))
            P.op("act", lambda e: e.copy(qkT[i][:], tps[:]), reads=[b_tps], writes=[b_qkT[i]])
            for hh in range(4):
                P.op("pe", lambda e, hh=hh: e.matmul(sps[:, hh, :], qkT[i][:, 4 + hh, :], qkT[i][:, hh, :], start=True, stop=True),
                     reads=[b_qkT[i]], writes=[b_sps], inc=(hh == 3))
            P.op("act", lambda e: e.copy(ssb[:], sps[:]), reads=[b_sps], writes=[b_ssb])
            P.op("dve", lambda e: e.tensor_tensor(sT[i][:], ssb[:], mask.unsqueeze(1).to_broadcast([128, 4, 128]), ALU.mult),
                 reads=[b_ssb, b_consts], writes=[b_sT[i]])

        def stage_b(cc):
            ci = cc if d == 0 else NT - 1 - cc
            i = cc % NB
            tok = slice(ci * 128, (ci + 1) * 128)
            so, sn_ = cc % 2, (cc + 1) % 2
            for hh in range(4):
                P.op("pe", lambda e, hh=hh: e.matmul(ups[:, hh, :], qk[i][:, 1, hh * 128:(hh + 1) * 128], gv_t[i][:, hh * 256:(hh + 1) * 256],
                                                    start=True, stop=True),
                     reads=[b_qk[i], b_in[i]], writes=[b_ups], inc=(hh == 3))
            for hp in range(2):
                P.op("dve", lambda e, hp=hp: e.tensor_tensor(St[:, 2 * hp:2 * hp + 2, :], St[:, 2 * hp:2 * hp + 2, :], ups[:, 2 * hp:2 * hp + 2, :], ALU.add),
                     reads=[b_ups, b_S], writes=[b_S])
            P.op("dve", lambda e: e.tensor_tensor(St[:], St[:], ebl[i][:].unsqueeze(2).to_broadcast([128, 4, 256]), ALU.mult),
                 reads=[b_ebl[i], b_S], writes=[b_S])
            P.op("act", lambda e: e.copy(Sb2[sn_][:], St[:]), reads=[b_S], writes=[b_Sb2[sn_]])
            for hh in range(4):
                P.op("pe", lambda e, hh=hh: e.matmul(ops[:, hh, :], sT[i][:, hh, :], gv_t[i][:, hh * 256:(hh + 1) * 256], start=True, stop=False),
                     reads=[b_sT[i], b_in[i]], writes=[b_ops], inc=False)
                P.op("pe", lambda e, hh=hh: e.matmul(ops[:, hh, :], qkT[i][:, hh, :], Sb2[so][:, hh, :], start=False, stop=True),
                     reads=[b_qkT[i], b_Sb2[so]], writes=[b_ops], inc=(hh == 3))
            if d == 0:
                for hp in range(2):
                    P.op("act", lambda e, hp=hp: e.copy(ostg[i][:, hp * 512:(hp + 1) * 512], ops[:, 2 * hp:2 * hp + 2, :].rearrange("p h e -> p (h e)")),
                         reads=[b_ops], writes=[b_ostg[i]])
                P.dma("pool", lambda e: e.dma_start(out=of_[tok, :], in_=ostg[i][:]), reads=[b_ostg[i]])
            else:
                for hp in range(2):
                    P.op("dve", lambda e, hp=hp: e.tensor_tensor(ostg[i][:, hp * 512:(hp + 1) * 512], ops[:, 2 * hp:2 * hp + 2, :].rearrange("p h e -> p (h e)"),
                                                                oprev[i][:, hp * 512:(hp + 1) * 512], ALU.add),
                         reads=[b_ops, b_oprev[i]], writes=[b_ostg[i]])

        def stage_c(cc):
            if d == 0:
                return
            ci = NT - 1 - cc
            i = cc % NB
            tok = slice(ci * 128, (ci + 1) * 128)
            o3 = ostg[i][:].rearrange("p (h e) -> p h e", h=4)
            sq3 = sqt[:].rearrange("p (h e) -> p h e", h=4)
            for hh in range(4):
                P.op("act", lambda e, hh=hh: e.activation(sqt[:, hh * 256:(hh + 1) * 256], ostg[i][:, hh * 256:(hh + 1) * 256], AF.Square, accum_out=ssn[i][:, hh:hh + 1]),
                     reads=[b_ostg[i]], writes=[b_sqt, b_ssn[i]])
            P.op("dve", lambda e: e.tensor_scalar(ssn[i][:, 4:8], ssn[i][:, 0:4], 1.0 / 256, EPS, ALU.mult, ALU.add), reads=[b_ssn[i]], writes=[b_ssn[i]])
            P.op("act", lambda e: e.activation(ssn[i][:, 8:12], ssn[i][:, 4:8], AF.Ln), reads=[b_ssn[i]], writes=[b_ssn[i]])
            P.op("act", lambda e: e.activation(ssn[i][:, 12:16], ssn[i][:, 8:12], AF.Exp, scale=-0.5), reads=[b_ssn[i]], writes=[b_ssn[i]])
            P.op("dve", lambda e: e.tensor_tensor(o3, o3, ssn[i][:, 12:16].unsqueeze(2).to_broadcast([128, 4, 256]), ALU.mult),
                 reads=[b_ssn[i], b_ostg[i]], writes=[b_ostg[i]])
            P.op("dve", lambda e: e.tensor_tensor(o3, o3, gg[:], ALU.mult), reads=[b_gg, b_ostg[i]], writes=[b_ostg[i]])
            P.op("dve", lambda e: e.tensor_tensor(ogb[i][:], ostg[i][:], grt[cc % 3][:], ALU.mult), reads=[b_ostg[i], b_grt[cc % 3]], writes=[b_ogb[i]])
            for j in range(8):
                P.op("pe", lambda e, j=j: e.transpose(gps[:, j, :], ogb[i][:, j * 128:(j + 1) * 128], ident_bf[:]),
                     reads=[b_ogb[i], b_identbf], writes=[b_gps], inc=(j == 7))
            P.op("act", lambda e: e.copy(ogs[i][:], gps[:]), reads=[b_gps], writes=[b_ogs[i]])
            P.dma("pool", lambda e: e.dma_start(out=ogT[:, :, tok].rearrange("c p n -> p c n"), in_=ogs[i][:]), reads=[b_ogs[i]])

        stage_a(0)
        for cc in range(NT):
            if cc + 1 < NT:
                stage_a(cc + 1)
            stage_b(cc)
            if cc >= 1:
                stage_c(cc - 1)
        stage_c(NT - 1)
        P.barrier(); P.emit(); Y.close()

    def phase_merge(l):
        [t._get() for t in (w_bd, w_bg)]
        X = Ctx(nc)
        wbd = X.sb([128, 8, D], BF16, "wbd"); wbg = X.sb([128, 8, D], BF16, "wbg")
        b_wd8 = [Buf() for _ in range(8)]; b_wg8 = [Buf() for _ in range(8)]
        for ec in range(8):
            P.dma("pool", lambda e, ec=ec: e.dma_start(out=wbd[:, ec, :], in_=w_bd[l][ec * 128:(ec + 1) * 128, :]), writes=[b_wd8[ec]])
            P.dma("pool", lambda e, ec=ec: e.dma_start(out=wbg[:, ec, :], in_=w_bg[l][ec * 128:(ec + 1) * 128, :]), writes=[b_wg8[ec]])
        odt = [X.sb([128, 8, 512], BF16, "odt") for _ in range(2)]; ogt = [X.sb([128, 8, 512], BF16, "ogt") for _ in range(2)]
        b_o = [Buf(), Buf()]
        gat = X.sb([128, 16, 512], BF16, "gat"); gbt = X.sb([128, 16, 512], BF16, "gbt"); b_g = Buf()
        mt = [X.sb([128, 16, 512], BF16, "mt") for _ in range(2)]; b_mt = [Buf(), Buf()]
        sa = [X.sb([128, 512], F32, "sa") for _ in range(2)]; sb_ = [X.sb([128, 512], F32, "sb") for _ in range(2)]
        b_sa = [Buf(), Buf()]; b_sb = [Buf(), Buf()]
        yd = [X.ps([128, 512], F32, "yd") for _ in range(2)]; yg = [X.ps([128, 512], F32, "yg") for _ in range(2)]
        b_yd = [Buf(), Buf()]; b_yg = [Buf(), Buf()]
        n = 0
        for tb in range(8):
            i = tb % 2
            tok = slice(tb * 512, (tb + 1) * 512)
            P.dma("sp", lambda e, i=i, tok=tok: e.dma_start(out=odt[i][:], in_=odT[:, :, tok].rearrange("c p n -> p c n")), writes=[b_o[i]])
            P.dma("sp", lambda e, i=i, tok=tok: e.dma_start(out=ogt[i][:], in_=ogT[:, :, tok].rearrange("c p n -> p c n")), writes=[b_o[i]])
            P.dma("sp", lambda e, tok=tok: e.dma_start(out=gat[:], in_=gabT[0:16, :, tok].rearrange("c p n -> p c n")), writes=[b_g])
            P.dma("sp", lambda e, tok=tok: e.dma_start(out=gbt[:], in_=gabT[16:32, :, tok].rearrange("c p n -> p c n")), writes=[b_g])
            for dc in range(16):
                j = n % 2; n += 1
                for ec in range(8):
                    P.op("pe", lambda e, j=j, ec=ec, dc=dc, i=i: e.matmul(yd[j][:], wbd[:, ec, dc * 128:(dc + 1) * 128], odt[i][:, ec, :],
                                                                        start=(ec == 0), stop=(ec == 7)),
                         reads=[b_wd8[ec], b_o[i]], writes=[b_yd[j]], inc=(ec == 7))
                for ec in range(8):
                    P.op("pe", lambda e, j=j, ec=ec, dc=dc, i=i: e.matmul(yg[j][:], wbg[:, ec, dc * 128:(dc + 1) * 128], ogt[i][:, ec, :],
                                                                        start=(ec == 0), stop=(ec == 7)),
                         reads=[b_wg8[ec], b_o[i]], writes=[b_yg[j]], inc=(ec == 7))
                P.op("act", lambda e, j=j, dc=dc: e.activation(sa[j][:], gat[:, dc, :], AF.Sigmoid), reads=[b_g], writes=[b_sa[j]])
                P.op("act", lambda e, j=j, dc=dc: e.activation(sb_[j][:], gbt[:, dc, :], AF.Sigmoid), reads=[b_g], writes=[b_sb[j]])
                P.op("dve", lambda e, j=j: e.tensor_tensor(sa[j][:], sa[j][:], yd[j][:], ALU.mult), reads=[b_yd[j], b_sa[j]], writes=[b_sa[j]])
                P.op("dve", lambda e, j=j: e.tensor_tensor(sb_[j][:], sb_[j][:], yg[j][:], ALU.mult), reads=[b_yg[j], b_sb[j]], writes=[b_sb[j]])
                P.op("dve", lambda e, j=j, i=i, dc=dc: e.tensor_tensor(mt[i][:, dc, :], sa[j][:], sb_[j][:], ALU.add),
                     reads=[b_sa[j], b_sb[j]], writes=[b_mt[i]])
            P.dma("pool", lambda e, i=i, tok=tok: e.dma_start(out=mT[:, :, tok].rearrange("c p n -> p c n"), in_=mt[i][:]), reads=[b_mt[i]])
        P.barrier(); P.emit(); X.close()

    def phase_outproj(l, xsrc):
        [t._get() for t in (w_out, x_in)]
        X = Ctx(nc)
        wo = X.sb([128, KC, D], BF16, "wo"); b_wo = [Buf() for _ in range(KC)]
        for kc in range(KC):
            P.dma("pool", lambda e, kc=kc: e.dma_start(out=wo[:, kc, :], in_=w_out[l][kc * 128:(kc + 1) * 128, :]), writes=[b_wo[kc]])
        G1b = X.sb([128, D], F32, "G1b"); A2b = X.sb([128, D], F32, "A2b"); B2b = X.sb([128, D], F32, "B2b"); b_ab = Buf()
        bcast_load("sp", G1b[:], b_ab, modrows[l, 2]); bcast_load("sp", A2b[:], b_ab, modrows[l, 3]); bcast_load("sp", B2b[:], b_ab, modrows[l, 4])
        mt = [X.sb([128, KC, 512], BF16, "mt") for _ in range(2)]; b_mt = [Buf(), Buf()]
        xt = [X.sb([128, D], F32, "xt") for _ in range(3)]; b_xt = [Buf() for _ in range(3)]
        tmp = [X.sb([128, 512], F32, "tmp") for _ in range(2)]; b_tm = [Buf(), Buf()]
        hbf = [X.sb([128, D], BF16, "hbf") for _ in range(3)]; b_hbf = [Buf() for _ in range(3)]
        h2s = [X.sb([128, KC, 512], BF16, "h2s") for _ in range(2)]; b_h2s = [Buf(), Buf()]
        junk = X.sb([128, D], BF16, "junk"); ss = [X.sb([128, 4], F32, "ss") for _ in range(3)]; b_ss = [Buf() for _ in range(3)]
        yps = [X.ps([128, 512], F32, "yps") for _ in range(3)]; b_yps = [Buf() for _ in range(3)]
        tp = [X.ps([128, KC, 128], BF16, "tp") for _ in range(2)]; b_tp = [Buf(), Buf()]
        ny = 0; nt_ = 0
        pend = []
        for tb in range(8):
            i = tb % 2
            tok = slice(tb * 512, (tb + 1) * 512)
            P.dma("sp", lambda e, i=i, tok=tok: e.dma_start(out=mt[i][:], in_=mT[:, :, tok].rearrange("c p n -> p c n")), writes=[b_mt[i]])
            for s in range(4):
                xi = nt_ % 3; nt_ += 1
                t0 = tb * 512 + s * 128
                P.dma("sp", lambda e, xi=xi, t0=t0: e.dma_start(out=xt[xi][:], in_=xsrc[t0:t0 + 128, :]), writes=[b_xt[xi]])
                for cg in range(4):
                    yi = ny % 3; ti = ny % 2; ny += 1
                    for kc in range(KC):
                        P.op("pe", lambda e, yi=yi, kc=kc, i=i, s=s, cg=cg: e.matmul(yps[yi][:], mt[i][:, kc, s * 128:(s + 1) * 128],
                                                                                   wo[:, kc, cg * 512:(cg + 1) * 512], start=(kc == 0), stop=(kc == KC - 1)),
                             reads=[b_mt[i], b_wo[kc]], writes=[b_yps[yi]], inc=(kc == KC - 1))
                    cs = slice(cg * 512, (cg + 1) * 512)
                    P.op("dve", lambda e, yi=yi, ti=ti, cs=cs: e.tensor_tensor(tmp[ti][:], yps[yi][:], G1b[:, cs], ALU.mult),
                         reads=[b_yps[yi], b_ab], writes=[b_tm[ti]])
                    P.op("dve", lambda e, xi=xi, ti=ti, cs=cs: e.tensor_tensor(xt[xi][:, cs], xt[xi][:, cs], tmp[ti][:], ALU.add),
                         reads=[b_tm[ti], b_xt[xi]], writes=[b_xt[xi]])
                P.dma("pool", lambda e, xi=xi, t0=t0: e.dma_start(out=xa[t0:t0 + 128, :], in_=xt[xi][:]), reads=[b_xt[xi]])
                rms_mod(xt[xi], b_xt[xi], A2b, B2b, b_ab, hbf[xi], b_hbf[xi], ss[xi], junk, b_ss[xi])

                def later(xi=xi, i=i, s=s, tb=tb, tok=tok):
                    pi = (tb * 4 + s) % 2
                    for kc in range(KC):
                        P.op("pe", lambda e, kc=kc: e.transpose(tp[pi][:, kc, :], hbf[xi][:, kc * 128:(kc + 1) * 128], ident_bf[:]),
                             reads=[b_hbf[xi], b_identbf], writes=[b_tp[pi]], inc=(kc == KC - 1))
                    P.op("act", lambda e: e.copy(h2s[i][:, :, s * 128:(s + 1) * 128], tp[pi][:]), reads=[b_tp[pi]], writes=[b_h2s[i]])
                    if s == 3:
                        P.dma("pool", lambda e: e.dma_start(out=h2T[:, :, tok].rearrange("c p n -> p c n"), in_=h2s[i][:]), reads=[b_h2s[i]])
                pend.append(later)
                if len(pend) > 1:
                    pend.pop(0)()
        for f in pend:
            f()
        P.barrier(); P.emit(); X.close()

    def phase_ffn_up(l):
        [t._get() for t in (conv_w, conv_b, w_gate, w_up)]
        X = Ctx(nc)
        HW = 2049
        hs = X.sb([128, KC, HW], BF16, "hs"); b_hs = Buf()
        cw = X.sb([128, 4, NFC], F32, "cw"); b_cw = Buf()
        for k in range(4):
            vec = conv_w[l, k] if k < 3 else conv_b[l]
            for q4 in range(4):
                P.dma("sp", lambda e, k=k, q4=q4, vec=vec: e.dma_start(
                    out=cw[:, k, q4 * 11:(q4 + 1) * 11], in_=vec[q4 * 1408:(q4 + 1) * 1408].rearrange("(c p) -> p c", p=128),
                    allow_slow_non_contiguous=True), writes=[b_cw])
        wg = [X.sb([128, KC, 512], BF16, "wg") for _ in range(2)]; wu = [X.sb([128, KC, 512], BF16, "wu") for _ in range(2)]
        b_wg4 = [[Buf() for _ in range(4)] for _ in range(2)]; b_wu4 = [[Buf() for _ in range(4)] for _ in range(2)]
        gfull = [X.sb([128, HW + 2], F32, "gfull") for _ in range(2)]; b_gf = [Buf(), Buf()]
        usb = [X.sb([128, HW], BF16, "usb") for _ in range(2)]; b_u = [Buf(), Buf()]
        tmp = [X.sb([128, 2048], F32, "tmp") for _ in range(2)]; b_tmp = [Buf(), Buf()]
        asb = [X.sb([128, 2048], BF16, "asb") for _ in range(2)]; b_a = [Buf(), Buf()]
        gps = [X.ps([128, 512], F32, "gps") for _ in range(2)]; ups = [X.ps([128, 512], F32, "ups") for _ in range(2)]
        b_gps = [Buf(), Buf()]; b_ups = [Buf(), Buf()]
        for i in range(2):
            P.op("dve", lambda e, i=i: e.memset(gfull[i][:], 0.0), writes=[b_gf[i]])
        blocks = [(0, 410), (410, 410), (820, 410), (1230, 410), (1640, 409)]
        nw = 0; nf = 0; npb = 0
        for hf in range(2):
            t_lo = 0 if hf == 0 else 2047
            P.dma("sp", lambda e, t_lo=t_lo: e.dma_start(out=hs[:], in_=h2T[:, :, t_lo:t_lo + HW].rearrange("c p n -> p c n")), writes=[b_hs])
            o_li = 0 if hf == 0 else 1
            for fg in range(NFC // 4):
                wi = nw % 2; nw += 1
                csl = slice(fg * 512, (fg + 1) * 512)
                for q4 in range(4):
                    P.dma("pool", lambda e, wi=wi, csl=csl, q4=q4: e.dma_start(out=wg[wi][:, q4 * 4:(q4 + 1) * 4, :],
                                                                               in_=w_gate[l][q4 * 512:(q4 + 1) * 512, csl].rearrange("(kc p) n -> p kc n", p=128)), writes=[b_wg4[wi][q4]])
                for q4 in range(4):
                    P.dma("pool", lambda e, wi=wi, csl=csl, q4=q4: e.dma_start(out=wu[wi][:, q4 * 4:(q4 + 1) * 4, :],
                                                                               in_=w_up[l][q4 * 512:(q4 + 1) * 512, csl].rearrange("(kc p) n -> p kc n", p=128)), writes=[b_wu4[wi][q4]])
                for j in range(4):
                    fc = fg * 4 + j
                    fi = nf % 2; nf += 1
                    for (c0, n) in blocks:
                        pi = npb % 2; npb += 1
                        for kc in range(KC):
                            P.op("pe", lambda e, pi=pi, kc=kc, wi=wi, j=j, c0=c0, n=n: e.matmul(gps[pi][:, 0:n], wg[wi][:, kc, j * 128:(j + 1) * 128], hs[:, kc, c0:c0 + n],
                                                                                              start=(kc == 0), stop=(kc == KC - 1)),
                                 reads=[b_wg4[wi][kc // 4], b_hs], writes=[b_gps[pi]], inc=(kc == KC - 1))
                        for kc in range(KC):
                            P.op("pe", lambda e, pi=pi, kc=kc, wi=wi, j=j, c0=c0, n=n: e.matmul(ups[pi][:, 0:n], wu[wi][:, kc, j * 128:(j + 1) * 128], hs[:, kc, c0:c0 + n],
                                                                                              start=(kc == 0), stop=(kc == KC - 1)),
                                 reads=[b_wu4[wi][kc // 4], b_hs], writes=[b_ups[pi]], inc=(kc == KC - 1))
                        P.op("act", lambda e, pi=pi, fi=fi, c0=c0, n=n: e.copy(gfull[fi][:, 1 + c0:1 + c0 + n], gps[pi][:, 0:n]), reads=[b_gps[pi]], writes=[b_gf[fi]])
                        P.op("dve", lambda e, pi=pi, fi=fi, c0=c0, n=n: e.tensor_copy(usb[fi][:, c0:c0 + n], ups[pi][:, 0:n]), reads=[b_ups[pi]], writes=[b_u[fi]])
                    p0 = o_li + 1
                    P.op("dve", lambda e, fi=fi, fc=fc, p0=p0: e.tensor_scalar(tmp[fi][:], gfull[fi][:, p0:p0 + 2048], cw[:, 1, fc:fc + 1], cw[:, 3, fc:fc + 1], ALU.mult, ALU.add),
                         reads=[b_gf[fi], b_cw], writes=[b_tmp[fi]])
                    P.op("dve", lambda e, fi=fi, fc=fc, p0=p0: e.scalar_tensor_tensor(tmp[fi][:], gfull[fi][:, p0 - 1:p0 - 1 + 2048], cw[:, 0, fc:fc + 1], tmp[fi][:], ALU.mult, ALU.add),
                         reads=[b_gf[fi], b_cw, b_tmp[fi]], writes=[b_tmp[fi]])
                    P.op("dve", lambda e, fi=fi, fc=fc, p0=p0: e.scalar_tensor_tensor(tmp[fi][:], gfull[fi][:, p0 + 1:p0 + 1 + 2048], cw[:, 2, fc:fc + 1], tmp[fi][:], ALU.mult, ALU.add),
                         reads=[b_gf[fi], b_cw, b_tmp[fi]], writes=[b_tmp[fi]])
                    P.op("act", lambda e, fi=fi: e.activation(tmp[fi][:], tmp[fi][:], AF.Silu), reads=[b_tmp[fi]], writes=[b_tmp[fi]])
                    P.op("dve", lambda e, fi=fi, o_li=o_li: e.tensor_tensor(asb[fi][:], tmp[fi][:], usb[fi][:, o_li:o_li + 2048], ALU.mult),
                         reads=[b_tmp[fi], b_u[fi]], writes=[b_a[fi]])
                    P.dma("sp", lambda e, fi=fi, fc=fc, hf=hf: e.dma_start(out=aT[fc, :, hf * 2048:(hf + 1) * 2048], in_=asb[fi][:]), reads=[b_a[fi]])
        P.barrier(); P.emit(); X.close()

    def phase_ffn_down(l, xsrc, xdst):
        w_down._get()
        X = Ctx(nc)
        wd = X.sb([128, NFC, 1024], BF16, "wd"); b_wd = [Buf() for _ in range(4)]
        G2b = X.sb([128, D], F32, "G2b"); b_g2 = Buf()
        bcast_load("sp", G2b[:], b_g2, modrows[l, 5])
        at = [X.sb([128, NFC, 256], BF16, "at") for _ in range(3)]; b_at = [Buf() for _ in range(3)]
        xt = [X.sb([128, 512], F32, "xt") for _ in range(3)]; b_xt = [Buf() for _ in range(3)]
        tmp = [X.sb([128, 512], F32, "tmp") for _ in range(2)]; b_tm = [Buf(), Buf()]
        yps = [X.ps([128, 512], F32, "yps") for _ in range(3)]; b_yps = [Buf() for _ in range(3)]
        n = 0
        for cb in range(2):
            for q4 in range(4):
                fsl = slice(q4 * 11, (q4 + 1) * 11)
                P.dma("pool", lambda e, cb=cb, q4=q4, fsl=fsl: e.dma_start(
                    out=wd[:, fsl, :], in_=w_down[l][q4 * 11 * 128:(q4 + 1) * 11 * 128, cb * 1024:(cb + 1) * 1024].rearrange("(fc p) n -> p fc n", p=128)),
                    writes=[b_wd[q4]])
            for tb in range(16):
                i = (cb * 16 + tb) % 3
                tok = slice(tb * 256, (tb + 1) * 256)
                P.dma("pool", lambda e, i=i, tok=tok: e.dma_start(out=at[i][:], in_=aT[:, :, tok].rearrange("c p n -> p c n")), writes=[b_at[i]])
                for s in range(2):
                    t0 = tb * 256 + s * 128
                    for cg in range(2):
                        yi = n % 3; ti = n % 2; xi = n % 3; n += 1
                        c0 = cb * 1024 + cg * 512
                        P.dma("sp", lambda e, xi=xi, t0=t0, c0=c0: e.dma_start(out=xt[xi][:], in_=xsrc[t0:t0 + 128, c0:c0 + 512]), writes=[b_xt[xi]])
                        for fc in range(NFC):
                            P.op("pe", lambda e, yi=yi, fc=fc, i=i, s=s, cg=cg: e.matmul(yps[yi][:], at[i][:, fc, s * 128:(s + 1) * 128],
                                                                                       wd[:, fc, cg * 512:(cg + 1) * 512], start=(fc == 0), stop=(fc == NFC - 1)),
                                 reads=[b_at[i], b_wd[fc // 11]], writes=[b_yps[yi]], inc=(fc == NFC - 1))
                        P.op("dve", lambda e, yi=yi, ti=ti, c0=c0: e.tensor_tensor(tmp[ti][:], yps[yi][:], G2b[:, c0:c0 + 512], ALU.mult),
                             reads=[b_yps[yi], b_g2], writes=[b_tm[ti]])
                        P.op("dve", lambda e, xi=xi, ti=ti: e.tensor_tensor(xt[xi][:], xt[xi][:], tmp[ti][:], ALU.add),
                             reads=[b_tm[ti], b_xt[xi]], writes=[b_xt[xi]])
                        P.dma("sp", lambda e, xi=xi, t0=t0, c0=c0: e.dma_start(out=xdst[t0:t0 + 128, c0:c0 + 512], in_=xt[xi][:]), reads=[b_xt[xi]])
        P.barrier(); P.emit(); X.close()

    def phase_final(xsrc):
        fin_g._get()
        X = Ctx(nc)
        gb = X.sb([128, D], F32, "gb"); b_gb = Buf()
        bcast_load("sp", gb[:], b_gb, fin_g[:])
        xt = [X.sb([128, D], F32, "xt") for _ in range(3)]; b_xt = [Buf() for _ in range(3)]
        junk = X.sb([128, D], BF16, "junk"); ss = [X.sb([128, 4], F32, "ss") for _ in range(3)]; b_ss = [Buf() for _ in range(3)]
        for t in range(NT):
            i = t % 3
            P.dma("sp", lambda e, i=i, t=t: e.dma_start(out=xt[i][:], in_=xsrc[t * 128:(t + 1) * 128, :]), writes=[b_xt[i]])
            rms_mod(xt[i], b_xt[i], gb, None, b_gb, None, None, ss[i], junk, b_ss[i])
            P.dma("pool", lambda e, i=i, t=t: e.dma_start(out=out[t * 128:(t + 1) * 128, :], in_=xt[i][:]), reads=[b_xt[i]])
        P.barrier(); P.emit(); X.close()

    P.barrier(); P.emit()
    phases = []
    phases.append(("rotary", phase_rotary))
    phases.append(("adaln", phase_adaln))
    for l in range(nlayers):
        xsrc = x_in if l == 0 else xb
        phases.append((f"inproj{l}", lambda l=l, xsrc=xsrc: phase_inproj(l, xsrc)))
        phases.append((f"diff{l}", lambda l=l: phase_diffattn(l)))
        phases.append((f"gla{l}", lambda l=l: phase_gla(l)))
        phases.append((f"merge{l}", lambda l=l: phase_merge(l)))
        phases.append((f"outproj{l}", lambda l=l, xsrc=xsrc: phase_outproj(l, xsrc)))
        phases.append((f"ffnup{l}", lambda l=l: phase_ffn_up(l)))
        phases.append((f"ffndown{l}", lambda l=l: phase_ffn_down(l, xa, xb)))
    phases.append(("final", lambda: phase_final(xb)))
    for name, fn in phases:
        if name in skip or (only is not None and name not in only):
            continue
        fn()
        if stop is not None and (name == stop or name + "a" == stop or name + "b" == stop):
            break
    G.close()
    nc._used_inputs = used_inputs + used_scratch_in
    return nc, P


def make_consts():
    c = np.zeros((128, NCONST), np.float32)
    j = np.arange(128)[:, None]
    i = np.arange(128)[None, :]
    c[:, C_ID:C_ID + 128] = np.eye(128, dtype=np.float32)
    c[:, C_TRIF:C_TRIF + 128] = np.where(j <= i, -1.0 / 16.0, 0.0)
    c[:, C_TRIB:C_TRIB + 128] = np.where(j >= i, -1.0 / 16.0, 0.0)
    c[:, C_MF:C_MF + 128] = (j <= i).astype(np.float32)
    c[:, C_MB:C_MB + 128] = (j >= i).astype(np.float32)
    c[127, C_OL] = 1.0
    c[0, C_OF] = 1.0
    inv_freq = (500000.0 ** (-np.arange(0, 32, 2, dtype=np.float32) / 32.0)).astype(np.float32)
    c[:, C_INVF:C_INVF + 16] = inv_freq[None, :]
    return c


_WNAMES = ["w_ada", "b_ada", "norm_mix_g", "w_in", "lambda_q1", "lambda_k1", "lambda_q2", "lambda_k2",
           "diff_subln_g", "gla_w2_fwd", "gla_b_fwd", "gla_w2_bwd", "gla_b_bwd", "gla_norm_g",
           "w_branch_diff", "w_branch_gla", "w_out", "norm_ffn_g", "w_gate", "w_up", "conv_w", "conv_b",
           "w_down", "final_norm_g"]


def make_in_maps(inputs):
    consts = make_consts()
    shared = {n: np.ascontiguousarray(np.asarray(inputs[n], dtype=np.float32)) for n in _WNAMES}
    x = np.asarray(inputs["x"], dtype=np.float32)
    c = np.asarray(inputs["c"], dtype=np.float32)
    pos = np.asarray(inputs["positions"]).astype(np.int32)
    in_maps = []
    for b in range(8):
        m = dict(shared)
        m["x"] = np.ascontiguousarray(x[b])
        m["c"] = np.ascontiguousarray(c[b].reshape(128, 16))
        m["pos"] = np.ascontiguousarray(pos[b].reshape(NT, 128).T)
        m["consts"] = consts
        in_maps.append(m)
    return in_maps


def kernel(**inputs):
    nc, _ = build()
    in_maps = [{k: v for k, v in m.items() if k in nc._used_inputs} for m in make_in_maps(inputs)]
    res = run_bass_kernel_spmd(nc, in_maps, core_ids=list(range(8)))
    return np.stack([np.asarray(r["out"], dtype=np.float32) for r in res.results], axis=0)
```
